# Optimizing a Trainium2 kernel written in Bass

```python
import math
import jax, jax.numpy as jnp
from jax import lax
import numpy as np

D_MODEL = 1024
BATCH = 16
SEQ = 2048
DEPTH = 1

HEAD_DIM = 64
A_Q_HEADS = 8
A_KV_HEADS = 2
B_HEADS = 8
GRID_W = 64
ROPE_THETA = 10000.0
Q_BLOCK = 128
DIL_BRANCHES = ((128, 1), (512, 4), (2048, 16))
DIL_BLOCK = max(w // (2 * r) for w, r in DIL_BRANCHES)
N_BUCKETS = 32
BUCKET_MAX_DIST = 1024
D_FF = 4 * D_MODEL
EPS = 1e-6
NEG = -1e30
A_WIDTH = A_Q_HEADS * HEAD_DIM
A_KV_WIDTH = A_KV_HEADS * HEAD_DIM
B_WIDTH = B_HEADS * HEAD_DIM
IN_WIDTH = A_WIDTH + 2 * A_KV_WIDTH + 3 * B_WIDTH
MIX_WIDTH = A_WIDTH + B_WIDTH

kernel_name = "hybrid_axial_gqa_dilated_attn_block"


def rms_norm(x, g):
    xf = x.astype(jnp.float32)
    y = xf * lax.rsqrt(jnp.mean(xf * xf, axis=-1, keepdims=True) + EPS)
    return (y * g.astype(jnp.float32)).astype(x.dtype)


def axial_rope_tables(S):
    ROWS = S // GRID_W
    row = jnp.repeat(jnp.arange(ROWS, dtype=jnp.float32), GRID_W)
    col = jnp.tile(jnp.arange(GRID_W, dtype=jnp.float32), ROWS)
    n_freq = HEAD_DIM // 4
    inv = ROPE_THETA ** (-jnp.arange(n_freq, dtype=jnp.float32) / n_freq)
    ang_r = row[:, None] * inv[None, :]
    ang_c = col[:, None] * inv[None, :]
    return jnp.cos(ang_r), jnp.sin(ang_r), jnp.cos(ang_c), jnp.sin(ang_c)


def rotate_half(x, cos, sin):
    x1, x2 = jnp.split(x, 2, axis=-1)
    return jnp.concatenate([x1 * cos - x2 * sin, x2 * cos + x1 * sin], axis=-1)


def axial_rope(x, tabs):
    cos_r, sin_r, cos_c, sin_c = tabs
    xf = x.astype(jnp.float32)
    a = HEAD_DIM // 2
    y = jnp.concatenate([rotate_half(xf[..., :a], cos_r, sin_r),
                         rotate_half(xf[..., a:], cos_c, sin_c)], axis=-1)
    return y.astype(x.dtype)


def gqa_axial_attention(q, k, v, gq, gk):
    B, S = q.shape[0], q.shape[1]
    tabs = axial_rope_tables(S)
    q = axial_rope(rms_norm(q, gq).transpose(0, 2, 1, 3), tabs)
    k = axial_rope(rms_norm(k, gk).transpose(0, 2, 1, 3), tabs)
    v = v.transpose(0, 2, 1, 3)
    G = A_Q_HEADS // A_KV_HEADS
    nb = S // Q_BLOCK
    q = q * (HEAD_DIM ** -0.5)
    qb = q.reshape(B, A_KV_HEADS, G, nb, Q_BLOCK, HEAD_DIM).transpose(3, 0, 1, 2, 4, 5)

    def block(qi):
        s = jnp.einsum('bkgqd,bksd->bkgqs', qi, k).astype(jnp.float32)
        p = jax.nn.softmax(s, axis=-1).astype(v.dtype)
        return jnp.einsum('bkgqs,bksd->bkgqd', p, v)

    o = lax.map(block, qb)
    return o.transpose(1, 0, 4, 2, 3, 5).reshape(B, S, A_WIDTH)


def t5_bucket(rel):
    nb_half = N_BUCKETS // 2
    max_exact = nb_half // 2
    ret = jnp.where(rel > 0, nb_half, 0)
    n = jnp.abs(rel)
    large = max_exact + (jnp.log(jnp.maximum(n, 1).astype(jnp.float32) / max_exact)
                         / math.log(BUCKET_MAX_DIST / max_exact)
                         * (nb_half - max_exact)).astype(jnp.int32)
    large = jnp.minimum(large, nb_half - 1)
    return ret + jnp.where(n < max_exact, n, large)


def band_blocks(xp, nb, qb):
    Bq, H, r, _, D = xp.shape
    parts = [xp[:, :, :, j * qb:j * qb + nb * qb].reshape(Bq, H, r, nb, qb, D) for j in range(3)]
    return jnp.concatenate(parts, axis=4)


def dilated_branch(q, k, v, rel_bias, window, dilation):
    B, H, S, D = q.shape
    r = dilation
    L = S // r
    half = window // (2 * r)
    Qb = DIL_BLOCK
    nb = -(-L // Qb)
    Lp = nb * Qb

    def to_phase(t):
        return t.reshape(B, H, L, r, D).transpose(0, 1, 3, 2, 4)

    qp = jnp.pad(to_phase(q), ((0, 0), (0, 0), (0, 0), (0, Lp - L), (0, 0))).reshape(B, H, r, nb, Qb, D)
    pad_kv = ((0, 0), (0, 0), (0, 0), (Qb, Lp - L + Qb), (0, 0))
    kb = band_blocks(jnp.pad(to_phase(k), pad_kv), nb, Qb)
    vb = band_blocks(jnp.pad(to_phase(v), pad_kv), nb, Qb)

    kk = jnp.arange(3 * Qb)
    qq = jnp.arange(Qb)
    rel = kk[None, :] - Qb - qq[:, None]
    kpos = jnp.arange(nb)[:, None] * Qb + kk[None, :] - Qb
    mask = (jnp.abs(rel) <= half)[None] & ((kpos >= 0) & (kpos < L))[:, None, :]
    bias = rel_bias[t5_bucket(rel * r)].transpose(2, 0, 1).astype(jnp.float32)

    s = jnp.einsum('bhrnqd,bhrnkd->bhrnqk', qp, kb).astype(jnp.float32) + bias[:, None, None]
    s = jnp.where(mask, s, NEG)
    m = jnp.max(s, axis=-1, keepdims=True)
    e = jnp.exp(s - m)
    den = jnp.sum(e, axis=-1, keepdims=True)
    o = jnp.einsum('bhrnqk,bhrnkd->bhrnqd', e, vb.astype(jnp.float32)) / den
    lse = (m + jnp.log(den))[..., 0]
    o = o.reshape(B, H, r, Lp, D)[:, :, :, :L].transpose(0, 1, 3, 2, 4).reshape(B, H, S, D)
    lse = lse.reshape(B, H, r, Lp)[:, :, :, :L].transpose(0, 1, 3, 2).reshape(B, H, S)
    return o, lse


def dilated_mixture_attention(q, k, v, rel_bias):
    B, S = q.shape[0], q.shape[1]
    q = q.transpose(0, 2, 1, 3) * (HEAD_DIM ** -0.5)
    k = k.transpose(0, 2, 1, 3)
    v = v.transpose(0, 2, 1, 3)
    outs, lses = [], []
    for window, dilation in DIL_BRANCHES:
        o, l = dilated_branch(q, k, v, rel_bias, window, dilation)
        outs.append(o)
        lses.append(l)
    alpha = jax.nn.softmax(jnp.stack(lses, axis=0), axis=0)
    y = jnp.einsum('nbhs,nbhsd->bhsd', alpha, jnp.stack(outs, axis=0))
    return y.transpose(0, 2, 1, 3).reshape(B, S, B_WIDTH).astype(v.dtype)


def setup_inputs(seed: int = 0) -> dict:
    key = jax.random.key(seed)
    ks = jax.random.split(key, 16)
    f32 = jnp.float32

    def gain(k, shape):
        return jnp.ones(shape, f32) + 0.05 * jax.random.normal(k, shape, f32)

    return {
        "x": jax.random.normal(ks[0], (BATCH, SEQ, D_MODEL), f32),
        "attn_norm_g": gain(ks[1], (DEPTH, D_MODEL)),
        "w_in": jax.random.normal(ks[2], (DEPTH, D_MODEL, IN_WIDTH), f32) * D_MODEL ** -0.5,
        "q_norm_g": gain(ks[3], (DEPTH, HEAD_DIM)),
        "k_norm_g": gain(ks[4], (DEPTH, HEAD_DIM)),
        "rel_bias": 0.5 * jax.random.normal(ks[5], (N_BUCKETS, B_HEADS), f32),
        "out_norm_a_g": gain(ks[6], (DEPTH, A_WIDTH)),
        "out_norm_b_g": gain(ks[7], (DEPTH, B_WIDTH)),
        "w_out": jax.random.normal(ks[8], (DEPTH, MIX_WIDTH, D_MODEL), f32) * MIX_WIDTH ** -0.5,
        "mlp_norm_g": gain(ks[9], (DEPTH, D_MODEL)),
        "w_up": jax.random.normal(ks[10], (DEPTH, D_MODEL, D_FF), f32) * D_MODEL ** -0.5,
        "w_down": jax.random.normal(ks[11], (DEPTH, D_FF, D_MODEL), f32) * D_FF ** -0.5,
        "final_norm_g": gain(ks[12], (D_MODEL,)),
    }


def reference(x, attn_norm_g, w_in, q_norm_g, k_norm_g, rel_bias, out_norm_a_g, out_norm_b_g,
              w_out, mlp_norm_g, w_up, w_down, final_norm_g):
    B, S, _ = x.shape
    h = x
    o1 = A_WIDTH
    o2 = o1 + A_KV_WIDTH
    o3 = o2 + A_KV_WIDTH
    o4 = o3 + B_WIDTH
    o5 = o4 + B_WIDTH
    for l in range(DEPTH):
        n = rms_norm(h, attn_norm_g[l])
        proj = jnp.einsum('bsd,de->bse', n, w_in[l])
        qa = proj[..., :o1].reshape(B, S, A_Q_HEADS, HEAD_DIM)
        ka = proj[..., o1:o2].reshape(B, S, A_KV_HEADS, HEAD_DIM)
        va = proj[..., o2:o3].reshape(B, S, A_KV_HEADS, HEAD_DIM)
        qb = proj[..., o3:o4].reshape(B, S, B_HEADS, HEAD_DIM)
        kb = proj[..., o4:o5].reshape(B, S, B_HEADS, HEAD_DIM)
        vb = proj[..., o5:].reshape(B, S, B_HEADS, HEAD_DIM)
        ya = gqa_axial_attention(qa, ka, va, q_norm_g[l], k_norm_g[l])
        yb = dilated_mixture_attention(qb, kb, vb, rel_bias)
        mix = jnp.concatenate([rms_norm(ya, out_norm_a_g[l]), rms_norm(yb, out_norm_b_g[l])], axis=-1)
        h = h + jnp.einsum('bse,ed->bsd', mix, w_out[l])
        u = jnp.einsum('bsd,df->bsf', rms_norm(h, mlp_norm_g[l]), w_up[l])
        h = h + jnp.einsum('bsf,fd->bsd', jnp.square(jax.nn.relu(u)), w_down[l])
    return rms_norm(h, final_norm_g)
```

```python
import math
import numpy as np
import concourse.bass as bass
import concourse.mybir as mybir
from concourse.bass_utils import run_bass_kernel_spmd

F32 = mybir.dt.float32
BF16 = mybir.dt.bfloat16
U8 = mybir.dt.uint8
AF = mybir.ActivationFunctionType
ALU = mybir.AluOpType
AX = mybir.AxisListType

NCORES = 8
S = 2048
D = 1024
NT = 16
DFF = 4096
EPS = 1e-6
HW = 2944
UW = 3072
ENGS = ["sync", "scalar", "gpsimd", "vector", "tensor"]


class Tok:
    __slots__ = ("sem", "val", "eng", "idx")

    def __init__(self, sem, val, eng, idx):
        self.sem, self.val, self.eng, self.idx = sem, val, eng, idx


class DmaSem:
    def __init__(self, nc, name):
        self.h = nc.alloc_semaphore(name)
        self.val = 0


class Sched:
    def __init__(self, nc):
        self.nc = nc
        self.q = {e: [] for e in ENGS}
        self.sem = {e: nc.alloc_semaphore("c_" + e) for e in ENGS}
        self.cnt = {e: 0 for e in ENGS}
        self.idx = {e: 0 for e in ENGS}
        self.waited = {e: {} for e in ENGS}

    def _flat(self, deps, acc):
        for t in deps:
            if t is None:
                continue
            if isinstance(t, (list, tuple)):
                self._flat(t, acc)
            else:
                k = t.sem.num
                if k not in acc or acc[k].val < t.val:
                    acc[k] = t

    def _waits(self, eng, deps):
        acc = {}
        self._flat(deps, acc)
        for key, t in acc.items():
            if self.waited[eng].get(key, 0) >= t.val:
                continue
            if t.eng == eng and t.idx is not None and self.idx[eng] - t.idx >= 3:
                continue
            self.waited[eng][key] = t.val
            self.q[eng].append(lambda e, s=t.sem, v=t.val: e.wait_ge(s, v))

    def op(self, eng, fn, deps=(), sig=True):
        self._waits(eng, deps)
        i = self.idx[eng]
        self.idx[eng] += 1
        if sig:
            self.cnt[eng] += 1
            sem, val = self.sem[eng], self.cnt[eng]
            self.q[eng].append(lambda e: fn(e).then_inc(sem, 1))
            return Tok(sem, val, eng, i)
        self.q[eng].append(lambda e: fn(e))
        return None

    def dma(self, eng, out, in_, dsem, deps=()):
        self._waits(eng, deps)
        self.idx[eng] += 1
        dsem.val += 16
        h = dsem.h
        self.q[eng].append(lambda e: e.dma_start(out=out, in_=in_).then_inc(h, 16))
        return Tok(h, dsem.val, "dma", None)

    def wait_all(self, eng, deps):
        self._waits(eng, deps)

    def flush(self):
        nc = self.nc
        qs = self.q
        with nc.Block() as block:
            @block.sync
            def _(e):
                for f in qs["sync"]:
                    f(e)

            @block.scalar
            def _(e):
                for f in qs["scalar"]:
                    f(e)

            @block.gpsimd
            def _(e):
                for f in qs["gpsimd"]:
                    f(e)

            @block.vector
            def _(e):
                for f in qs["vector"]:
                    f(e)

            @block.tensor
            def _(e):
                for f in qs["tensor"]:
                    f(e)
        self.q = {e: [] for e in ENGS}


class Ring:
    def __init__(self, bufs):
        self.bufs = bufs
        self.free = [[] for _ in bufs]
        self.i = 0

    def get(self):
        i = self.i
        self.i = (i + 1) % len(self.bufs)
        fr = self.free[i]
        self.free[i] = []
        return i, self.bufs[i], fr

    def release(self, i, tok):
        if isinstance(tok, (list, tuple)):
            self.free[i].extend(tok)
        else:
            self.free[i].append(tok)


def build_nc(debug=False):
    nc = bass.Bass("TRN2", target_bir_lowering=False)
    x = nc.dram_tensor("x", [2, S, D], F32, kind="ExternalInput").ap()
    w_in = nc.dram_tensor("w_in", [D, 2304], F32, kind="ExternalInput").ap()
    w_out = nc.dram_tensor("w_out", [D, D], F32, kind="ExternalInput").ap()
    w_up = nc.dram_tensor("w_up", [D, DFF], F32, kind="ExternalInput").ap()
    w_dn = nc.dram_tensor("w_dn", [DFF, D], F32, kind="ExternalInput").ap()
    small_d = nc.dram_tensor("small", [128, 32], F32, kind="ExternalInput").ap()
    gfin_d = nc.dram_tensor("gfin", [1, D], F32, kind="ExternalInput").ap()
    cmat_d = nc.dram_tensor("cmat", [128, 512], F32, kind="ExternalInput").ap()
    ropeC_d = nc.dram_tensor("ropeC", [128, S], F32, kind="ExternalInput").ap()
    ropeS_d = nc.dram_tensor("ropeS", [128, S], F32, kind="ExternalInput").ap()
    rb33_d = nc.dram_tensor("rb33", [33, 8], F32, kind="ExternalInput").ap()
    oh33_d = nc.dram_tensor("oh33", [33, UW], F32, kind="ExternalInput").ap()
    out = nc.dram_tensor("out", [2, S, D], F32, kind="ExternalOutput").ap()
    vscr = nc.dram_tensor("vscr", [8, UW], BF16)
    hscr = nc.dram_tensor("hscr", [2, S, D], F32, kind="ExternalOutput" if debug else "Internal").ap()
    if debug:
        dbg = {
            "nT": nc.dram_tensor("dbg_nT", [128, 8 * S], BF16, kind="ExternalOutput").ap(),
            "QAT": nc.dram_tensor("dbg_QAT", [128, 4 * S], BF16, kind="ExternalOutput").ap(),
            "KAT": nc.dram_tensor("dbg_KAT", [128, S], BF16, kind="ExternalOutput").ap(),
            "QBT": nc.dram_tensor("dbg_QBT", [128, 4 * S], BF16, kind="ExternalOutput").ap(),
            "KBT": nc.dram_tensor("dbg_KBT", [128, 4 * S], BF16, kind="ExternalOutput").ap(),
            "VA": nc.dram_tensor("dbg_VA", [128, NT * 2 * 65], BF16, kind="ExternalOutput").ap(),
            "VB": nc.dram_tensor("dbg_VB", [128, NT * 8 * 65], BF16, kind="ExternalOutput").ap(),
            "Y": nc.dram_tensor("dbg_Y", [128, NT * 1024], BF16, kind="ExternalOutput").ap(),
            "ssqh": nc.dram_tensor("dbg_ssqh", [128, 256], F32, kind="ExternalOutput").ap(),
            "Hk": nc.dram_tensor("dbg_Hk", [128, HW], BF16, kind="ExternalOutput").ap(),
            "sm": nc.dram_tensor("dbg_sm", [128, 512], F32, kind="ExternalOutput").ap(),
        }
        d_dbg = None

    ARENA = 212800
    arena = nc.alloc_sbuf_tensor("arena", [128, ARENA], U8)

    def view(off, n, dt, pat=None, **kw):
        esz = 4 if dt == F32 else 2
        assert off % 32 == 0 and off + n * esz <= ARENA, (off, n)
        v = arena[:, off:off + n * esz].bitcast(dt)
        if pat:
            v = v.rearrange(pat, **kw)
        return v

    class Alloc:
        def __init__(self, base):
            self.off = base

        def take(self, nbytes):
            o = self.off
            self.off += (nbytes + 31) // 32 * 32
            return o

    al = Alloc(0)
    cb = view(al.take(1024), 512, BF16, "p (a b) -> p a b", a=4)
    ident, Jm, onesblk, Rmat = cb[:, 0, :], cb[:, 1, :], cb[:, 2, :], cb[:, 3, :]
    sm = view(al.take(2048), 512, F32)
    GIN, GOUT, GMLP, GQ, GK, EPSC, GQ8 = 0, 8, 16, 24, 25, 26, 27
    MLP_BASE = al.off
    Win = view(al.take(8 * 2304 * 2), 8 * 2304, BF16, "p (c n) -> p c n", c=8)
    QAT = view(al.take(4 * S * 2), 4 * S, BF16, "p (c n) -> p c n", c=4)
    KAT = view(al.take(S * 2), S, BF16)
    QBT = view(al.take(4 * S * 2), 4 * S, BF16, "p (c n) -> p c n", c=4)
    KBT = view(al.take(4 * S * 2), 4 * S, BF16, "p (c n) -> p c n", c=4)
    VA = view(al.take(NT * 2 * 65 * 2), NT * 2 * 65, BF16, "p (t h e) -> p t h e", t=NT, h=2)
    VB = view(al.take(NT * 8 * 65 * 2), NT * 8 * 65, BF16, "p (t h e) -> p t h e", t=NT, h=8)
    WOUT_OFF = al.off
    Wout = view(al.take(8 * 1024 * 2), 8 * 1024, BF16, "p (c n) -> p c n", c=8)
    R0 = al.off
    ar = Alloc(R0)
    nT = view(ar.take(8 * S * 2), 8 * S, BF16, "p (c n) -> p c n", c=8)
    ropeCb = [view(ar.take(2048), 512, F32) for _ in range(2)]
    ropeSb = [view(ar.take(2048), 512, F32) for _ in range(2)]
    NXB = 3
    NXS = 4
    xs = [view(ar.take(4096), 1024, F32) for _ in range(NXS)]
    xnb = [view(ar.take(2048), 1024, BF16) for _ in range(NXB)]
    sqb = [view(ar.take(1024), 512, BF16) for _ in range(2)]
    rtb = [view(ar.take(2048), 512, F32) for _ in range(2)]
    rinv = rtb
    qnb = [view(ar.take(1024), 512, BF16) for _ in range(2)]
    t1b = [view(ar.take(2048), 512, F32) for _ in range(2)]
    t2b = [view(ar.take(2048), 512, F32) for _ in range(2)]
    assert ar.off <= ARENA, ar.off
    aa = Alloc(R0)
    Hk = [view(aa.take(HW * 2), HW, BF16) for _ in range(4)]
    Y = view(aa.take(NT * 1024 * 2), NT * 1024, BF16, "p (t n) -> p t n", t=NT)
    NP = 4
    Pb = [view(aa.take(2048), 1024, BF16) for _ in range(NP)]
    yfb = [view(aa.take(1024), 256, F32, "p (a b) -> p a b", a=4) for _ in range(2)]
    ysq = view(aa.take(1024), 256, F32, "p (a b) -> p a b", a=4)
    rdenb = [view(aa.take(32), 4, F32) for _ in range(2)]
    ocp = [view(aa.take(1056), 260, F32, "p (a b) -> p a b", b=65) for _ in range(2)]
    ssqh = view(aa.take(2 * 8 * 16 * 4), 2 * 8 * 16, F32, "p (k h t) -> p k h t", k=2, h=8)
    assert aa.off <= ARENA, aa.off
    Y_off_end = R0 + 4 * ((HW * 2 + 31) // 32 * 32) + NT * 1024 * 2
    ap_ = Alloc(R0)
    mixT = [view(ap_.take(2048), 1024, BF16, "p (c n) -> p c n", c=8) for _ in range(2)]
    xr = [view(ap_.take(4096), 1024, F32) for _ in range(2)]
    hst = [view(ap_.take(4096), 1024, F32) for _ in range(2)]
    junkb = view(ap_.take(1024), 512, BF16)
    assert ap_.off <= R0 + 4 * (HW * 2)
    QKV0 = 2048 + 1024 + 8 * 2304 * 2
    as_ = Alloc(QKV0)
    cmat_f = view(as_.take(2048), 512, F32)
    rb33 = view(as_.take(32), 8, F32)
    oh33 = view(as_.take(UW * 4), UW, F32)
    vsb = view(as_.take(UW * 2), UW, BF16)
    as2 = Alloc(QKV0)
    NHB = 4
    hkst = [view(as2.take(HW * 2), HW, BF16) for _ in range(8)]
    tkst = [view(as2.take(HW * 2), HW, BF16) for _ in range(NHB)]
    assert as2.off <= WOUT_OFF
    assert as_.off <= R0
    am = Alloc(MLP_BASE)
    Wup = view(am.take(8 * DFF * 2), 8 * DFF, BF16, "p (c n) -> p c n", c=8)
    Wdn = view(am.take(32 * D * 2), 32 * D, BF16, "p (c n) -> p c n", c=32)
    gfin = view(am.take(4096), 1024, F32)
    G = 256
    aT = view(am.take(32 * G * 2), 32 * G, BF16, "p (c n) -> p c n", c=32)
    hnT = [view(am.take(8 * G * 2), 8 * G, BF16, "p (c n) -> p c n", c=8) for _ in range(2)]
    NHT = 6
    hT = [view(am.take(4096), 1024, F32) for _ in range(NHT)]
    hnb = [view(am.take(2048), 1024, BF16) for _ in range(2)]
    rst = [view(am.take(1024), G, F32) for _ in range(2)]
    ost = [view(am.take(4096), 1024, F32) for _ in range(2)]
    junkm = view(am.take(2048), 1024, BF16)
    assert am.off <= ARENA, am.off

    psall = nc.alloc_psum_tensor("psall", [128, 4096], F32)

    def psb(i):
        return psall[:, i * 512:(i + 1) * 512]

    def psbf(i):
        return psall[:, i * 512:(i + 1) * 512].bitcast(BF16)

    sc = Sched(nc)
    ndma = [0]

    def dsem():
        ndma[0] += 1
        return DmaSem(nc, "d%d" % ndma[0])

    t_small = sc.dma("sync", sm[:, 0:32], small_d, dsem())
    t_cm = sc.dma("sync", cmat_f, cmat_d, dsem())
    t_rb = sc.dma("sync", rb33[0:33, :], rb33_d, dsem())
    t_oh = sc.dma("sync", oh33[0:33, :], oh33_d, dsem())
    t_cb = sc.op("vector", lambda e: e.tensor_copy(out=cb.rearrange("p a b -> p (a b)"), in_=cmat_f), [t_cm])
    t_eps = sc.op("vector", lambda e: e.memset(sm[:, EPSC:EPSC + 1], EPS), [t_small])
    t_gq8 = sc.op("vector", lambda e: e.tensor_scalar_mul(out=sm[:, GQ8:GQ8 + 1], in0=sm[:, GQ:GQ + 1], scalar1=0.125), [t_small])
    t_v = []
    for j in range(UW // 512):
        tm = sc.op("tensor", lambda e, j=j: e.matmul(psb(j)[0:8, :], lhsT=rb33[0:33, :], rhs=oh33[0:33, j * 512:(j + 1) * 512],
                                                     start=True, stop=True), [t_rb, t_oh])
        t_v.append(sc.op("vector", lambda e, j=j: e.tensor_copy(out=vsb[0:8, j * 512:(j + 1) * 512], in_=psb(j)[0:8, :]), [tm]))
    d_v = dsem()
    t_vscr = sc.dma("sync", vscr.ap(), vsb[0:8, :], d_v, t_v)
    ps_free = [[t_v[j]] if j < UW // 512 else [] for j in range(8)]
    d_win = dsem()
    t_win = None
    for c in range(8):
        t_win = sc.dma("gpsimd", Win[:, c, :], w_in[c * 128:(c + 1) * 128, :], d_win)
    t_win = [t_win]
    d_wout = dsem()
    t_wout = None
    for c in range(8):
        t_wout = sc.dma("gpsimd", Wout[:, c, :], w_out[c * 128:(c + 1) * 128, :], d_wout, t_win)
    t_wout = [t_wout]
    setup_done = t_win + [t_vscr, t_cb, t_eps, t_gq8] + t_v

    d_x = [dsem() for _ in range(4)]
    d_ropeC = [dsem() for _ in range(2)]
    d_ropeS = [dsem() for _ in range(2)]
    d_hk = [dsem() for _ in range(4)]
    d_xr = [dsem() for _ in range(2)]
    d_hs = [dsem() for _ in range(2)]

    region_free = list(setup_done)
    SX, RX, TX = 128, 160, 192
    SA, RA, TA = 224, 256, 288

    for s in range(2):
        psr = Ring([3, 4, 5, 6, 7])
        psr.free = [list(ps_free[b]) for b in psr.bufs]
        psr_p = Ring([0, 1])
        psr_p.free = [list(ps_free[b]) for b in psr_p.bufs]
        psr_s = Ring([2])
        psr_s.free = [list(ps_free[b]) for b in psr_s.bufs]
        xfree = [list(region_free) if s > 0 else [] for _ in range(NXS)]
        xnfree = [list(region_free) for _ in range(NXB)]
        t_nT = [None] * NT
        qk_tokens, a_tokens = [], []
        t_vones = sc.op("gpsimd", lambda e: e.memset(VA[:, :, :, 64:65], 1.0), [region_free])
        t_vones2 = sc.op("gpsimd", lambda e: e.memset(VB[:, :, :, 64:65], 1.0), [region_free])
        v_tokens = [t_vones, t_vones2]
        ropefree = [list(region_free), list(region_free)]
        rope_tok = {}
        rope_users = {0: [], 1: [], 2: [], 3: []}
        ab_free = {"sq": [[], []], "rt": [[], []], "ri": [[], []], "qn": [[], []], "t1": [[], []], "t2": [[], []]}
        acount = [0]

        def tg_dep(tg):
            r = t_nT[tg * 4:(tg + 1) * 4]
            assert all(t is not None for t in r), tg
            return r

        def tile_unit(i):
            b = i % NXB
            bx = i % NXS
            tl = sc.dma("sync", xs[bx], x[s, i * 128:(i + 1) * 128, :], d_x[bx], xfree[bx])
            ta = sc.op("scalar", lambda e: e.activation(out=xnb[b], in_=xs[bx], func=AF.Square,
                                                        accum_out=sm[:, SX + i:SX + i + 1]), [tl, xnfree[b]])
            tb_ = sc.op("scalar", lambda e: e.activation(out=sm[:, TX + i:TX + i + 1], in_=sm[:, SX + i:SX + i + 1], func=AF.Ln,
                                                         bias=sm[:, EPSC:EPSC + 1], scale=1.0 / D), [ta, t_eps])
            tr = sc.op("scalar", lambda e: e.activation(out=sm[:, RX + i:RX + i + 1], in_=sm[:, TX + i:TX + i + 1], func=AF.Exp,
                                                        scale=-0.5), [tb_])
            tn = sc.op("vector", lambda e: e.tensor_scalar_mul(out=xnb[b], in0=xs[bx], scalar1=sm[:, RX + i:RX + i + 1]), [tr, ta])
            xfree[bx] = [tn]
            yield
            bi, bank, fr = psr.get()
            tt = None
            for c in range(8):
                tt = sc.op("tensor", lambda e, c=c: e.transpose(psbf(bank)[:, c * 128:(c + 1) * 128],
                                                                xnb[b][:, c * 128:(c + 1) * 128], ident),
                           [tn, fr, t_cb, region_free], sig=(c == 7))
            xnfree[b] = [tt]
            te = sc.op("vector", lambda e: e.tensor_tensor(
                out=nT[:, :, i * 128:(i + 1) * 128], in0=psbf(bank).rearrange("p (c n) -> p c n", c=8),
                in1=sm[:, GIN:GIN + 8].unsqueeze(2).to_broadcast([128, 8, 128]), op=ALU.mult), [tt, t_small])
            psr.release(bi, te)
            t_nT[i] = te
            yield

        def qkb_unit(j, tg):
            isq = j < 4
            col0 = 640 + j * 128 if isq else 1152 + (j - 4) * 128
            bi, bank, fr = psr.get()
            tm = None
            for c in range(8):
                tm = sc.op("tensor", lambda e, c=c: e.matmul(
                    psb(bank), lhsT=Win[:, c, col0:col0 + 128], rhs=nT[:, c, tg * 512:(tg + 1) * 512],
                    start=(c == 0), stop=(c == 7)), [tg_dep(tg), fr, t_win, region_free], sig=(c == 7))
            dst = (QBT if isq else KBT)[:, j % 4, tg * 512:(tg + 1) * 512]
            if (j + tg) % 2 == 0:
                te = sc.op("vector", lambda e: e.tensor_scalar_mul(out=dst, in0=psb(bank), scalar1=0.125 if isq else 1.0), [tm])
            else:
                te = sc.op("scalar", lambda e: e.activation(out=dst, in_=psb(bank), func=AF.Copy, scale=0.125 if isq else 1.0), [tm])
            psr.release(bi, te)
            qk_tokens.append(te)
            yield

        def v_unit(i, which):
            assert t_nT[i] is not None
            bi, bank, fr = psr.get()
            tm = None
            if which == "B":
                for c in range(8):
                    tm = sc.op("tensor", lambda e, c=c: e.matmul(
                        psb(bank), lhsT=nT[:, c, i * 128:(i + 1) * 128], rhs=Win[:, c, 1792:2304],
                        start=(c == 0), stop=(c == 7)), [t_nT[i], fr, t_win, region_free], sig=(c == 7))
                src = psb(bank).rearrange("p (h d) -> p h d", h=8)
                dst = VB[:, i, :, 0:64]
                tv = t_vones2
            else:
                for c in range(8):
                    tm = sc.op("tensor", lambda e, c=c: e.matmul(
                        psb(bank)[:, 0:128], lhsT=nT[:, c, i * 128:(i + 1) * 128], rhs=Win[:, c, 1664:1792],
                        start=(c == 0), stop=(c == 7)), [t_nT[i], fr, t_win, region_free], sig=(c == 7))
                src = psb(bank)[:, 0:128].rearrange("p (h d) -> p h d", h=2)
                dst = VA[:, i, :, 0:64]
                tv = t_vones
            if i % 2 == 0:
                te = sc.op("vector", lambda e: e.tensor_copy(out=dst, in_=src), [tm, tv])
            else:
                te = sc.op("scalar", lambda e: e.activation(out=dst, in_=src, func=AF.Copy), [tm, tv])
            psr.release(bi, te)
            v_tokens.append(te)
            yield

        def a_unit(j, tg):
            if tg not in rope_tok:
                rbk_ = tg % 2
                rope_tok[tg] = (sc.dma("sync", ropeCb[rbk_], ropeC_d[:, tg * 512:(tg + 1) * 512], d_ropeC[rbk_], ropefree[rbk_]),
                                sc.dma("sync", ropeSb[rbk_], ropeS_d[:, tg * 512:(tg + 1) * 512], d_ropeS[rbk_], ropefree[rbk_]))
            rbk = tg % 2
            tlc, tls = rope_tok[tg]
            col0 = j * 128 if j < 4 else 512
            gcol = GQ8 if j < 4 else GK
            dst = QAT[:, j, tg * 512:(tg + 1) * 512] if j < 4 else KAT[:, tg * 512:(tg + 1) * 512]
            k = acount[0] % 2
            acount[0] += 1
            bi, bank, fr = psr_p.get()
            tm = None
            for c in range(8):
                tm = sc.op("tensor", lambda e, c=c: e.matmul(
                    psb(bank), lhsT=Win[:, c, col0:col0 + 128], rhs=nT[:, c, tg * 512:(tg + 1) * 512],
                    start=(c == 0), stop=(c == 7)), [tg_dep(tg), fr, t_win, region_free], sig=(c == 7))
            tsq = sc.op("scalar", lambda e: e.activation(out=sqb[k], in_=psb(bank), func=AF.Square), [tm, ab_free["sq"][k]])
            yield
            b2i, bank2, fr2 = psr_s.get()
            tss = sc.op("tensor", lambda e: e.matmul(psb(bank2), lhsT=onesblk, rhs=sqb[k], start=True, stop=True), [tsq, fr2, t_cb])
            ab_free["sq"][k] = [tss]
            trt = sc.op("scalar", lambda e: e.activation(out=rtb[k], in_=psb(bank2), func=AF.Ln,
                                                         bias=sm[:, EPSC:EPSC + 1], scale=1.0 / 64), [tss, ab_free["rt"][k], ab_free["ri"][k], t_eps])
            psr_s.release(b2i, trt)
            tri = sc.op("scalar", lambda e: e.activation(out=rinv[k], in_=rtb[k], func=AF.Exp, scale=-0.5), [trt, ab_free["ri"][k]])
            ab_free["rt"][k] = [tri]
            tqn = sc.op("vector", lambda e: e.scalar_tensor_tensor(
                out=qnb[k], in0=psb(bank), scalar=sm[:, gcol:gcol + 1], in1=rinv[k], op0=ALU.mult, op1=ALU.mult),
                [tri, tm, ab_free["qn"][k], t_gq8])
            ab_free["ri"][k] = [tqn]
            psr_p.release(bi, tqn)
            yield
            b3i, bank3, fr3 = psr_s.get()
            trot = sc.op("tensor", lambda e: e.matmul(psb(bank3), lhsT=Rmat, rhs=qnb[k], start=True, stop=True), [tqn, fr3, t_cb])
            tt1 = sc.op("gpsimd", lambda e: e.tensor_tensor(out=t1b[k], in0=qnb[k], in1=ropeCb[rbk], op=ALU.mult),
                        [tqn, tlc, ab_free["t1"][k]])
            tt2 = sc.op("vector", lambda e: e.tensor_tensor(out=t2b[k], in0=psb(bank3), in1=ropeSb[rbk], op=ALU.mult),
                        [trot, tls, ab_free["t2"][k]])
            psr_s.release(b3i, tt2)
            ab_free["qn"][k] = [trot, tt1]
            tfin = sc.op("gpsimd", lambda e: e.tensor_tensor(out=dst, in0=t1b[k], in1=t2b[k], op=ALU.add), [tt1, tt2])
            ab_free["t1"][k] = [tfin]
            ab_free["t2"][k] = [tfin]
            a_tokens.append(tfin)
            rope_users[tg].extend([tt1, tt2])
            if len(rope_users[tg]) == 10:
                ropefree[rbk] = list(rope_users[tg])
            yield

        def rdy_tg(tg):
            return lambda: all(t is not None for t in t_nT[4 * tg:4 * tg + 4])

        def rdy_tile(i):
            return lambda: t_nT[i] is not None

        def rdy_xbuf(i):
            return lambda: i < NXB or t_nT[i - NXB] is not None

        pending = [(tile_unit(i), False, rdy_xbuf(i)) for i in range(4)]
        for tg in range(4):
            nxt_tiles = [(tile_unit(i), False, rdy_xbuf(i)) for i in range(4 * (tg + 1), 4 * (tg + 2))] if tg < 3 else []
            items = []
            for j in range(4):
                items += [(a_unit(j, tg), True, rdy_tg(tg)), (qkb_unit(j, tg), False, rdy_tg(tg)), (qkb_unit(4 + j, tg), False, rdy_tg(tg))]
                if j < len(nxt_tiles):
                    items.append(nxt_tiles[j])
                items += [(v_unit(4 * tg + j, "B"), False, rdy_tile(4 * tg + j))]
            items += [(a_unit(4, tg), True, rdy_tg(tg))] + [(v_unit(4 * tg + j, "A"), False, rdy_tile(4 * tg + j)) for j in range(4)]
            pending += items
        active = []
        WIN = 9
        while pending or active:
            while pending and len(active) < WIN:
                nA = sum(1 for a_ in active if a_[1])
                pick = None
                for idx_, it_ in enumerate(pending):
                    if it_[2]() and not (it_[1] and nA >= 2):
                        pick = idx_
                        break
                    if idx_ > 12:
                        break
                if pick is None:
                    break
                active.append(pending.pop(pick))
            assert active, "unit scheduler stuck"
            for a_ in list(active):
                try:
                    next(a_[0])
                except StopIteration:
                    active.remove(a_)
        proj_done = qk_tokens + v_tokens + a_tokens
        if debug and s == 0:
            d_dbg = dsem()
            fl = lambda v, p: v.rearrange(p)
            dt_ = [sc.dma("sync", dbg["nT"], fl(nT, "p c n -> p (c n)"), dsem(), proj_done),
                   sc.dma("sync", dbg["QAT"], fl(QAT, "p c n -> p (c n)"), dsem(), proj_done),
                   sc.dma("sync", dbg["KAT"], KAT, dsem(), proj_done),
                   sc.dma("sync", dbg["QBT"], fl(QBT, "p c n -> p (c n)"), dsem(), proj_done),
                   sc.dma("sync", dbg["KBT"], fl(KBT, "p c n -> p (c n)"), dsem(), proj_done),
                   sc.dma("sync", dbg["VA"], fl(VA, "p t h e -> p (t h e)"), dsem(), proj_done),
                   sc.dma("sync", dbg["VB"], fl(VB, "p t h e -> p (t h e)"), dsem(), proj_done)]
            proj_done = proj_done + dt_
        ps_free = [[] for _ in range(8)]
        for rg in (psr, psr_p, psr_s):
            for bidx, fr_ in zip(rg.bufs, rg.free):
                ps_free[bidx] = list(fr_)
        if s == 1:
            d_up_early = [dsem() for _ in range(4)]
            early_up = {}
            for q4 in range(4):
                tk = None
                for c in range(4):
                    tk = sc.dma("gpsimd", Wup[:, c, q4 * 1024:(q4 + 1) * 1024],
                                w_up[c * 128:(c + 1) * 128, q4 * 1024:(q4 + 1) * 1024], d_up_early[q4], proj_done)
                early_up[q4] = tk

        NSB = 3
        S_free = [list(ps_free[2 * b]) + list(ps_free[2 * b + 1]) for b in range(NSB)]
        O_free = [list(ps_free[6]), list(ps_free[7])]
        P_free = [list(proj_done) for _ in range(NP)]
        hk_free = [list(proj_done) for _ in range(4)]
        yf_free = [list(proj_done), list(proj_done)]
        ysq_free = [list(proj_done)]
        rden_free = [list(proj_done), list(proj_done)]
        ocp_free = [list(proj_done), list(proj_done)]
        y_tokens = []
        ssq_tokens = []
        hk_dma = {}
        hk_tok = {}
        obank = [6, 7]

        def load_hk(pair):
            for half in range(2):
                slot = (2 * pair + half) % 4
                h = 2 * pair + half
                hk_tok[(pair, half)] = sc.dma("sync", Hk[slot], bass.AP(vscr, h * UW, [[1, 128], [1, HW]]), d_hk[slot],
                                              [hk_free[slot], t_vscr])

        steps = []
        for kind in ("A", "B"):
            for pair in range(4):
                for qc in range(4):
                    kts = list(range(16)) if kind == "A" else list(range(max(0, 4 * qc - 8), min(15, 4 * qc + 11) + 1))
                    for n, kt in enumerate(kts):
                        steps.append(dict(kind=kind, pair=pair, qc=qc, kt=kt, first=(n == 0), last=(n == len(kts) - 1)))

        def KT(st, half):
            kt = st["kt"]
            if st["kind"] == "A":
                return KAT[half * 64:(half + 1) * 64, kt * 128:(kt + 1) * 128]
            return KBT[half * 64:(half + 1) * 64, st["pair"], kt * 128:(kt + 1) * 128]

        def QT(st, half):
            qc = st["qc"]
            src = QAT if st["kind"] == "A" else QBT
            return src[half * 64:(half + 1) * 64, st["pair"], qc * 512:(qc + 1) * 512]

        def Vt(st, half):
            if st["kind"] == "A":
                return VA[:, st["kt"], half, :]
            return VB[:, st["kt"], 2 * st["pair"] + half, :]

        qk_tok = {}

        def emit_qk(i):
            st = steps[i]
            buf = i % NSB
            deps = [S_free[buf], proj_done]
            S_free[buf] = []
            t = None
            isB = st["kind"] == "B"
            for half in range(2):
                t = sc.op("tensor", lambda e, half=half, buf=buf, st=st: e.matmul(
                    psb(2 * buf + half), lhsT=KT(st, half), rhs=QT(st, half), start=True, stop=(not isB)), deps,
                    sig=(half == 1 and not isB))
            if isB:
                kt, qc, pair = st["kt"], st["qc"], st["pair"]
                c0 = 1408 - (kt * 128 - qc * 512)
                assert 0 <= c0 and c0 + 512 <= HW
                for half in range(2):
                    slot = (2 * pair + half) % 4
                    t = sc.op("tensor", lambda e, half=half, buf=buf, slot=slot, c0=c0: e.matmul(
                        psb(2 * buf + half), lhsT=Jm, rhs=Hk[slot][:, c0:c0 + 512], start=False, stop=True),
                        [hk_tok[(pair, half)]], sig=(half == 1))
                st["mul"] = t
            qk_tok[i] = t

        def epilogue(st):
            kind, pair, qc = st["kind"], st["pair"], st["qc"]
            heads = [pair, pair + 4] if kind == "A" else [2 * pair, 2 * pair + 1]
            tcps = []
            for half in range(2):
                O3 = psb(obank[half])[:, 0:260].rearrange("p (a b) -> p a b", b=65)
                tcp = sc.op("vector", lambda e, half=half, O3=O3: e.tensor_copy(out=ocp[half], in_=O3), [st["pv"][half], ocp_free[half]])
                O_free[half] = [tcp]
                tcps.append(tcp)
            trds = []
            for half in range(2):
                trds.append(sc.op("vector", lambda e, half=half: e.reciprocal(out=rdenb[half], in_=ocp[half][:, :, 64]),
                                  [tcps[half], rden_free[half]]))
            for half in range(2):
                h = heads[half]
                colb = h * 64 + (0 if kind == "A" else 512)
                ty = sc.op("gpsimd", lambda e, half=half, colb=colb, qc=qc: e.tensor_tensor(
                    out=Y[:, qc * 4:(qc + 1) * 4, colb:colb + 64], in0=ocp[half][:, :, 0:64],
                    in1=rdenb[half].unsqueeze(2).to_broadcast([128, 4, 64]), op=ALU.mult), [trds[half], proj_done])
                ocp_free[half] = [ty]
                rden_free[half] = [ty]
                y_tokens.append(ty)

        load_hk(0)
        load_hk(1)
        LOOK = 2
        for i in range(min(LOOK, len(steps))):
            emit_qk(i)
        for i, st in enumerate(steps):
            if i + LOOK < len(steps):
                emit_qk(i + LOOK)
            buf = i % NSB
            pi = i % NP
            tex = sc.op("scalar", lambda e, buf=buf, pi=pi: e.activation(
                out=Pb[pi], in_=psall[:, buf * 1024:(buf + 1) * 1024], func=AF.Exp), [qk_tok[i], P_free[pi]])
            S_free[buf] = [tex]
            tready = tex
            tpv = None
            st["pv"] = [None, None]
            for half in range(2):
                for qb in range(4):
                    deps = [tready]
                    if st["first"]:
                        deps.append(O_free[half])
                    tpv = sc.op("tensor", lambda e, half=half, pi=pi, qb=qb, st=st: e.matmul(
                        psb(obank[half])[:, qb * 65:(qb + 1) * 65], lhsT=Pb[pi][:, half * 512 + qb * 128:half * 512 + (qb + 1) * 128],
                        rhs=Vt(st, half), start=(st["first"] and qb == 0), stop=st["last"], skip_group_check=True),
                        deps, sig=(qb == 3))
                if st["first"]:
                    O_free[half] = []
                st["pv"][half] = tpv
            P_free[pi] = [tpv]
            if st["last"]:
                epilogue(st)
                if st["kind"] == "B" and st["qc"] == 3:
                    pair = st["pair"]
                    for half in range(2):
                        slot = (2 * pair + half) % 4
                        hk_free[slot] = [st["mul"]]
                    if pair + 2 < 4:
                        load_hk(pair + 2)
        attn_done = y_tokens + [Tok(sc.sem["tensor"], sc.cnt["tensor"], "tensor", sc.idx["tensor"])]
        if debug and s == 0:
            dt_ = [sc.dma("sync", dbg["Y"], Y.rearrange("p t n -> p (t n)"), dsem(), attn_done),
                   sc.dma("sync", dbg["Hk"], Hk[3], dsem(), attn_done)]
            attn_done = attn_done + dt_

        if s == 1:
            FLATE = (WOUT_OFF - (MLP_BASE + 8 * DFF * 2)) // 2048
            d_dn_e = [dsem() for _ in range(4)]
            early_dn = {}
            for f8 in range(4):
                tk = None
                for f in range(f8 * 8, min(f8 * 8 + 8, FLATE)):
                    tk = sc.dma("gpsimd", Wdn[:, f, :], w_dn[f * 128:(f + 1) * 128, :], d_dn_e[f8], attn_done)
                early_dn[f8] = tk
            d_up7 = [dsem() for _ in range(4)]
            early_up7 = {}
            for q4 in range(4):
                tk = None
                for c in range(4, 8):
                    tk = sc.dma("gpsimd", Wup[:, c, q4 * 1024:(q4 + 1) * 1024],
                                w_up[c * 128:(c + 1) * 128, q4 * 1024:(q4 + 1) * 1024], d_up7[q4], attn_done)
                early_up7[q4] = tk

        psr_t = Ring([0, 1])
        psr_t.free = [list(attn_done) for _ in range(2)]
        psr = Ring([2, 3, 4, 5, 6, 7])
        psr.free = [list(attn_done) for _ in range(6)]
        mix_free = [list(attn_done), list(attn_done)]
        xr_free = [list(attn_done), list(attn_done)]
        hst_free = [list(attn_done), list(attn_done)]
        store_toks = []
        tstate = {}

        def p4_T(i):
            b = i % 2
            tlx = sc.dma("sync", xr[b], x[s, i * 128:(i + 1) * 128, :], d_xr[b], xr_free[b])
            tq_ = []
            for ab in range(2):
                tq_.append(sc.op("scalar", lambda e, ab=ab: e.activation(
                    out=junkb, in_=Y[:, i, ab * 512:(ab + 1) * 512], func=AF.Square,
                    accum_out=sm[:, SA + 2 * i + ab:SA + 2 * i + ab + 1]), [y_tokens, attn_done]))
            tln = sc.op("scalar", lambda e: e.activation(out=sm[:, TA + 2 * i:TA + 2 * i + 2], in_=sm[:, SA + 2 * i:SA + 2 * i + 2], func=AF.Ln,
                                                         bias=sm[:, EPSC:EPSC + 1], scale=1.0 / 512), [tq_])
            trs = sc.op("scalar", lambda e: e.activation(out=sm[:, RA + 2 * i:RA + 2 * i + 2], in_=sm[:, TA + 2 * i:TA + 2 * i + 2], func=AF.Exp,
                                                         scale=-0.5), [tln])
            bi, bank, fr = psr_t.get()
            tt = None
            for c in range(8):
                tt = sc.op("tensor", lambda e, c=c: e.transpose(psbf(bank)[:, c * 128:(c + 1) * 128],
                                                                Y[:, i, c * 128:(c + 1) * 128], ident),
                           [y_tokens, fr], sig=(c == 7))
            te = sc.op("vector", lambda e: e.tensor_tensor(
                out=mixT[b], in0=psbf(bank).rearrange("p (c n) -> p c n", c=8),
                in1=sm[:, GOUT:GOUT + 8].unsqueeze(2).to_broadcast([128, 8, 128]), op=ALU.mult), [tt, mix_free[b]])
            psr_t.release(bi, te)
            tstate[i] = (tlx, trs, te)

        def p4_M(i):
            b = i % 2
            tlx, trs, te = tstate[i]
            last_mm = None
            prev = []
            for half in range(2):
                banks = []
                for ab in range(2):
                    bi2, bank2, fr2 = psr.get()
                    tm = None
                    for cc in range(4):
                        c = ab * 4 + cc
                        tm = sc.op("tensor", lambda e, c=c, half=half, bank2=bank2, cc=cc: e.matmul(
                            psb(bank2), lhsT=mixT[b][:, c, :], rhs=Wout[:, c, half * 512:(half + 1) * 512],
                            start=(cc == 0), stop=(cc == 3)), [te, fr2, t_wout], sig=(cc == 3))
                    banks.append((bi2, bank2, tm))
                    last_mm = tm
                (ba_i, ba, tma), (bb_i, bb, tmb) = banks
                hs = hst[b][:, half * 512:(half + 1) * 512]
                th1 = sc.op("vector", lambda e, ba=ba, hs=hs, half=half: e.scalar_tensor_tensor(
                    out=hs, in0=psb(ba), scalar=sm[:, RA + 2 * i:RA + 2 * i + 1], in1=xr[b][:, half * 512:(half + 1) * 512],
                    op0=ALU.mult, op1=ALU.add), [tma, tlx, trs, hst_free[b]])
                psr.release(ba_i, th1)
                th2 = sc.op("vector", lambda e, bb=bb, hs=hs: e.scalar_tensor_tensor(
                    out=hs, in0=psb(bb), scalar=sm[:, RA + 2 * i + 1:RA + 2 * i + 2], in1=hs, op0=ALU.mult, op1=ALU.add),
                    [tmb, th1])
                psr.release(bb_i, th2)
                prev.append(th2)
            mix_free[b] = [last_mm]
            xr_free[b] = list(prev)
            tst = sc.dma("sync", hscr[s, i * 128:(i + 1) * 128, :], hst[b], d_hs[b], prev)
            hst_free[b] = [tst]
            store_toks.append(tst)

        p4_T(0)
        for i in range(NT):
            if i + 1 < NT:
                p4_T(i + 1)
            p4_M(i)
        ps_free = [[] for _ in range(8)]
        for rg in (psr, psr_t):
            for bidx, fr_ in zip(rg.bufs, rg.free):
                ps_free[bidx] = list(fr_)
        region_free = store_toks + [Tok(sc.sem["tensor"], sc.cnt["tensor"], "tensor", sc.idx["tensor"])] + \
            [Tok(sc.sem["vector"], sc.cnt["vector"], "vector", sc.idx["vector"])]

    if debug:
        region_free = region_free + [sc.dma("sync", dbg["sm"], sm, dsem(), region_free)]
    all_done = region_free + [Tok(sc.sem[e], sc.cnt[e], e, sc.idx[e]) for e in ("scalar", "gpsimd")]
    d_g = dsem()
    d_h = [dsem() for _ in range(6)]
    d_o = [dsem() for _ in range(2)]
    t_gfin = sc.dma("sync", gfin, gfin_d.partition_broadcast(128), d_g, all_done)
    sc.wait_all("gpsimd", all_done)
    d_up = [dsem() for _ in range(4)]
    d_dn = [dsem() for _ in range(4)]
    t_wup = [[None] * 4 for _ in range(8)]
    t_wdn = [None] * 32
    for q4 in range(4):
        for c in range(8):
            t_wup[c][q4] = early_up[q4] if c < 4 else early_up7[q4]
    d_dn_late = dsem()
    tk = None
    for f in range(FLATE, 32):
        tk = sc.dma("gpsimd", Wdn[:, f, :], w_dn[f * 128:(f + 1) * 128, :], d_dn_late, all_done)
    for f in range(32):
        t_wdn[f] = early_dn[f // 8] if f < FLATE else tk

    SH, RH, TH = 320, 352, 384
    SO, RO, TO = 416, 448, 480
    psr = Ring(list(range(8)))
    psr.free = [list(all_done) for _ in range(8)]
    h_free = [list(all_done) for _ in range(6)]
    hn_free = [list(all_done), list(all_done)]
    hnT_free = [list(all_done), list(all_done)]
    rst_free = [list(all_done), list(all_done)]
    ost_free = [list(all_done), list(all_done)]
    hcount = [0]
    NG = 2 * S // G
    TPG = G // 128
    final_toks = []

    def load_h(gt):
        s_, i_ = divmod(gt, NT)
        k = hcount[0] % 6
        hcount[0] += 1
        t = sc.dma("sync", hT[k], hscr[s_, i_ * 128:(i_ + 1) * 128, :], d_h[k], h_free[k])
        return k, t

    def prepA(g):
        st_ = []
        for tl_ in range(TPG):
            gt = g * TPG + tl_
            k, tl = load_h(gt)
            b = gt % 2
            ta = sc.op("scalar", lambda e, b=b, k=k, gt=gt: e.activation(out=hnb[b], in_=hT[k], func=AF.Square,
                                                                         accum_out=sm[:, SH + gt:SH + gt + 1]), [tl, hn_free[b]])
            tb_ = sc.op("scalar", lambda e, gt=gt: e.activation(out=sm[:, TH + gt:TH + gt + 1], in_=sm[:, SH + gt:SH + gt + 1], func=AF.Ln,
                                                                bias=sm[:, EPSC:EPSC + 1], scale=1.0 / D), [ta])
            tr = sc.op("scalar", lambda e, gt=gt: e.activation(out=sm[:, RH + gt:RH + gt + 1], in_=sm[:, TH + gt:TH + gt + 1], func=AF.Exp,
                                                               scale=-0.5), [tb_])
            tn = sc.op("vector", lambda e, b=b, k=k, gt=gt: e.tensor_scalar_mul(out=hnb[b], in0=hT[k], scalar1=sm[:, RH + gt:RH + gt + 1]),
                       [tr, ta])
            h_free[k] = [tn]
            st_.append((b, tn))
        return st_

    def prepB(g, st_):
        gb = g % 2
        toks = []
        for tl_, (b, tn) in enumerate(st_):
            bi, bank, fr = psr.get()
            tt = None
            for c in range(8):
                tt = sc.op("tensor", lambda e, c=c, b=b, bank=bank: e.transpose(psbf(bank)[:, c * 128:(c + 1) * 128],
                                                                                hnb[b][:, c * 128:(c + 1) * 128], ident),
                           [tn, fr], sig=(c == 7))
            hn_free[b] = [tt]
            te = sc.op("vector", lambda e, gb=gb, tl_=tl_, bank=bank: e.tensor_tensor(
                out=hnT[gb][:, :, tl_ * 128:(tl_ + 1) * 128], in0=psbf(bank).rearrange("p (c n) -> p c n", c=8),
                in1=sm[:, GMLP:GMLP + 8].unsqueeze(2).to_broadcast([128, 8, 128]), op=ALU.mult), [tt, hnT_free[gb]])
            psr.release(bi, te)
            toks.append(te)
        return toks

    hn_ready = prepB(0, prepA(0))
    for g in range(NG):
        gb = g % 2
        nxtA = None
        up_toks = []
        last_up = None
        for f in range(32):
            if f == 16 and g + 1 < NG:
                nxtA = prepA(g + 1)
            bi, bank, fr = psr.get()
            tm = None
            for c in range(8):
                tm = sc.op("tensor", lambda e, c=c, f=f, gb=gb, bank=bank: e.matmul(
                    psb(bank)[:, 0:G], lhsT=Wup[:, c, f * 128:(f + 1) * 128], rhs=hnT[gb][:, c, :],
                    start=(c == 0), stop=(c == 7)), [hn_ready, fr, t_wup[c][f // 8]], sig=(c == 7))
            k = f % 2
            trl = sc.op("scalar", lambda e, k=k, bank=bank: e.activation(out=rst[k], in_=psb(bank)[:, 0:G], func=AF.Relu), [tm, rst_free[k]])
            psr.release(bi, trl)
            tsq2 = sc.op("gpsimd", lambda e, k=k, f=f: e.tensor_tensor(out=aT[:, f, :], in0=rst[k], in1=rst[k], op=ALU.mult), [trl])
            rst_free[k] = [tsq2]
            up_toks.append(tsq2)
            last_up = tm
        hnT_free[gb] = [last_up]
        nxt_ready = prepB(g + 1, nxtA) if g + 1 < NG else None
        for tl_ in range(TPG):
            gt = g * TPG + tl_
            s_, i_ = divmod(gt, NT)
            k, tlh = load_h(gt)
            ob = gt % 2
            halves = []
            for half in range(2):
                bi, bank, fr = psr.get()
                tm = None
                for f in range(32):
                    tm = sc.op("tensor", lambda e, f=f, tl_=tl_, half=half, bank=bank: e.matmul(
                        psb(bank), lhsT=aT[:, f, tl_ * 128:(tl_ + 1) * 128], rhs=Wdn[:, f, half * 512:(half + 1) * 512],
                        start=(f == 0), stop=(f == 31)), [up_toks[f], fr, t_wdn[f]], sig=(f == 31))
                to = sc.op("vector", lambda e, ob=ob, k=k, half=half, bank=bank: e.tensor_tensor(
                    out=ost[ob][:, half * 512:(half + 1) * 512], in0=psb(bank), in1=hT[k][:, half * 512:(half + 1) * 512], op=ALU.add),
                    [tm, tlh, ost_free[ob]])
                psr.release(bi, to)
                halves.append(to)
            h_free[k] = list(halves)
            ta = sc.op("scalar", lambda e, ob=ob, gt=gt: e.activation(out=junkm, in_=ost[ob], func=AF.Square,
                                                                      accum_out=sm[:, SO + gt:SO + gt + 1]), [halves, all_done])
            tb_ = sc.op("scalar", lambda e, gt=gt: e.activation(out=sm[:, TO + gt:TO + gt + 1], in_=sm[:, SO + gt:SO + gt + 1], func=AF.Ln,
                                                                bias=sm[:, EPSC:EPSC + 1], scale=1.0 / D), [ta])
            tr = sc.op("scalar", lambda e, gt=gt: e.activation(out=sm[:, RO + gt:RO + gt + 1], in_=sm[:, TO + gt:TO + gt + 1], func=AF.Exp,
                                                               scale=-0.5), [tb_])
            tf = sc.op("vector", lambda e, ob=ob, gt=gt: e.scalar_tensor_tensor(
                out=ost[ob], in0=ost[ob], scalar=sm[:, RO + gt:RO + gt + 1], in1=gfin, op0=ALU.mult, op1=ALU.mult), [tr, t_gfin, halves])
            tst = sc.dma("sync", out[s_, i_ * 128:(i_ + 1) * 128, :], ost[ob], d_o[ob], [tf])
            ost_free[ob] = [tst]
            final_toks.append(tst)
        hn_ready = nxt_ready

    sc.wait_all("sync", final_toks)
    sc.flush()
    return nc


def _t5_bucket(rel):
    rel = np.asarray(rel, np.int64)
    nb_half, max_exact = 16, 8
    ret = np.where(rel > 0, nb_half, 0)
    n = np.abs(rel)
    lg = (np.log(np.maximum(n, 1).astype(np.float32) / np.float32(max_exact)) / np.float32(math.log(1024 / max_exact))
          * np.float32(nb_half - max_exact)).astype(np.int32)
    large = np.minimum(max_exact + lg, nb_half - 1)
    return ret + np.where(n < max_exact, n, large)


def _constants():
    ident = np.eye(128, dtype=np.float32)
    J = ident[::-1].copy()
    ob = np.zeros((128, 128), np.float32)
    ob[:64, :64] = 1
    ob[64:, 64:] = 1
    R = np.zeros((128, 128), np.float32)
    for m in range(128):
        d = m % 64
        base = m - d
        blk = d // 32
        dd = d % 32
        if dd < 16:
            R[base + blk * 32 + dd + 16, m] = -1.0
        else:
            R[base + blk * 32 + dd - 16, m] = 1.0
    cmat = np.concatenate([ident, J, ob, R], axis=1).astype(np.float32)
    t = np.arange(S)
    row = (t // 64).astype(np.float64)
    col = (t % 64).astype(np.float64)
    inv = (10000.0 ** (-np.arange(16, dtype=np.float32) / np.float32(16))).astype(np.float32).astype(np.float64)
    C = np.zeros((128, S), np.float32)
    Sn = np.zeros((128, S), np.float32)
    for p in range(128):
        d = p % 64
        pos = row if d < 32 else col
        f = (d % 32) % 16
        ang = (pos.astype(np.float32) * np.float32(inv[f])).astype(np.float32)
        C[p] = np.cos(ang.astype(np.float64))
        Sn[p] = np.sin(ang.astype(np.float64))
    u = np.arange(UW)
    d = 1535 - u
    ad = np.abs(d)
    mult = (ad <= 64).astype(np.int64) + ((d % 4 == 0) & (ad <= 256)) + ((d % 16 == 0) & (ad <= 1024))
    mult[u >= 3071] = 0
    bkt = _t5_bucket(d)
    oh = np.zeros((33, UW), np.float32)
    valid = mult > 0
    oh[bkt[valid], u[valid]] = 1.0
    oh[32] = np.where(valid, np.log(np.maximum(mult, 1)), -30000.0)
    return cmat, C, Sn, oh


_CACHE = {}


def kernel(x, attn_norm_g, w_in, q_norm_g, k_norm_g, rel_bias, out_norm_a_g, out_norm_b_g,
           w_out, mlp_norm_g, w_up, w_down, final_norm_g):
    x = np.asarray(x, np.float32)
    f = lambda a: np.ascontiguousarray(np.asarray(a, np.float32))
    w_in0 = f(w_in)[0]
    cols = []
    for i in range(4):
        cols += list(range(64 * i, 64 * i + 64)) + list(range(64 * (i + 4), 64 * (i + 4) + 64))
    cols += list(range(512, 640))
    cols += list(range(768, 1280))
    cols += list(range(1280, 1792))
    cols += list(range(640, 768))
    cols += list(range(1792, 2304))
    w_in_p = np.ascontiguousarray(w_in0[:, np.array(cols)])
    small = np.zeros((128, 32), np.float32)
    small[:, 0:8] = f(attn_norm_g)[0].reshape(8, 128).T
    small[:, 8:16] = np.concatenate([f(out_norm_a_g)[0], f(out_norm_b_g)[0]]).reshape(8, 128).T
    small[:, 16:24] = f(mlp_norm_g)[0].reshape(8, 128).T
    small[:, 24] = np.tile(f(q_norm_g)[0], 2)
    small[:, 25] = np.tile(f(k_norm_g)[0], 2)
    rb33 = np.concatenate([f(rel_bias), np.ones((1, 8), np.float32)], axis=0)
    if "c" not in _CACHE:
        _CACHE["c"] = _constants()
    cmat, C, Sn, oh = _CACHE["c"]
    nc = build_nc()
    shared = {
        "w_in": w_in_p, "w_out": f(w_out)[0], "w_up": f(w_up)[0], "w_dn": f(w_down)[0],
        "small": small, "gfin": f(final_norm_g).reshape(1, D), "cmat": cmat, "ropeC": C, "ropeS": Sn,
        "rb33": rb33, "oh33": oh,
    }
    in_maps = []
    for c in range(NCORES):
        m = dict(shared)
        m["x"] = np.ascontiguousarray(x[2 * c:2 * c + 2])
        in_maps.append(m)
    res = run_bass_kernel_spmd(nc, in_maps, core_ids=list(range(NCORES)))
    return np.concatenate([r["out"] for r in res.results], axis=0).astype(np.float32)
```

```python
import math
import numpy as np
import concourse.bass as bass
import concourse.mybir as mybir
from concourse.bass_utils import run_bass_kernel_spmd

F32 = mybir.dt.float32
BF16 = mybir.dt.bfloat16
U8 = mybir.dt.uint8
AF = mybir.ActivationFunctionType
ALU = mybir.AluOpType
AX = mybir.AxisListType

NCORES = 8
S = 2048
D = 1024
NT = 16
DFF = 4096
EPS = 1e-6
HW = 2944
UW = 3072
ENGS = ["sync", "scalar", "gpsimd", "vector", "tensor"]


class Tok:
    __slots__ = ("sem", "val", "eng", "idx")

    def __init__(self, sem, val, eng, idx):
        self.sem, self.val, self.eng, self.idx = sem, val, eng, idx


class DmaSem:
    def __init__(self, nc, name):
        self.h = nc.alloc_semaphore(name)
        self.val = 0


class Sched:
    def __init__(self, nc):
        self.nc = nc
        self.q = {e: [] for e in ENGS}
        self.sem = {e: nc.alloc_semaphore("c_" + e) for e in ENGS}
        self.cnt = {e: 0 for e in ENGS}
        self.idx = {e: 0 for e in ENGS}
        self.waited = {e: {} for e in ENGS}

    def _flat(self, deps, acc):
        for t in deps:
            if t is None:
                continue
            if isinstance(t, (list, tuple)):
                self._flat(t, acc)
            else:
                k = t.sem.num
                if k not in acc or acc[k].val < t.val:
                    acc[k] = t

    def _waits(self, eng, deps):
        acc = {}
        self._flat(deps, acc)
        for key, t in acc.items():
            if self.waited[eng].get(key, 0) >= t.val:
                continue
            if t.eng == eng and t.idx is not None and self.idx[eng] - t.idx >= 3:
                continue
            self.waited[eng][key] = t.val
            self.q[eng].append(lambda e, s=t.sem, v=t.val: e.wait_ge(s, v))

    def op(self, eng, fn, deps=(), sig=True):
        self._waits(eng, deps)
        i = self.idx[eng]
        self.idx[eng] += 1
        if sig:
            self.cnt[eng] += 1
            sem, val = self.sem[eng], self.cnt[eng]
            self.q[eng].append(lambda e: fn(e).then_inc(sem, 1))
            return Tok(sem, val, eng, i)
        self.q[eng].append(lambda e: fn(e))
        return None

    def dma(self, eng, out, in_, dsem, deps=()):
        self._waits(eng, deps)
        self.idx[eng] += 1
        dsem.val += 16
        h = dsem.h
        self.q[eng].append(lambda e: e.dma_start(out=out, in_=in_).then_inc(h, 16))
        return Tok(h, dsem.val, "dma", None)

    def wait_all(self, eng, deps):
        self._waits(eng, deps)

    def flush(self):
        nc = self.nc
        qs = self.q
        with nc.Block() as block:
            @block.sync
            def _(e):
                for f in qs["sync"]:
                    f(e)

            @block.scalar
            def _(e):
                for f in qs["scalar"]:
                    f(e)

            @block.gpsimd
            def _(e):
                for f in qs["gpsimd"]:
                    f(e)

            @block.vector
            def _(e):
                for f in qs["vector"]:
                    f(e)

            @block.tensor
            def _(e):
                for f in qs["tensor"]:
                    f(e)
        self.q = {e: [] for e in ENGS}


class Ring:
    def __init__(self, bufs):
        self.bufs = bufs
        self.free = [[] for _ in bufs]
        self.i = 0

    def get(self):
        i = self.i
        self.i = (i + 1) % len(self.bufs)
        fr = self.free[i]
        self.free[i] = []
        return i, self.bufs[i], fr

    def release(self, i, tok):
        if isinstance(tok, (list, tuple)):
            self.free[i].extend(tok)
        else:
            self.free[i].append(tok)


def build_nc(debug=False):
    nc = bass.Bass("TRN2", target_bir_lowering=False)
    x = nc.dram_tensor("x", [2, S, D], F32, kind="ExternalInput").ap()
    w_in = nc.dram_tensor("w_in", [D, 2304], F32, kind="ExternalInput").ap()
    w_out = nc.dram_tensor("w_out", [D, D], F32, kind="ExternalInput").ap()
    w_up = nc.dram_tensor("w_up", [D, DFF], F32, kind="ExternalInput").ap()
    w_dn = nc.dram_tensor("w_dn", [DFF, D], F32, kind="ExternalInput").ap()
    small_d = nc.dram_tensor("small", [128, 32], F32, kind="ExternalInput").ap()
    gfin_d = nc.dram_tensor("gfin", [1, D], F32, kind="ExternalInput").ap()
    cmat_d = nc.dram_tensor("cmat", [128, 512], F32, kind="ExternalInput").ap()
    ropeC_d = nc.dram_tensor("ropeC", [128, S], F32, kind="ExternalInput").ap()
    ropeS_d = nc.dram_tensor("ropeS", [128, S], F32, kind="ExternalInput").ap()
    rb33_d = nc.dram_tensor("rb33", [33, 8], F32, kind="ExternalInput").ap()
    oh33_d = nc.dram_tensor("oh33", [33, UW], F32, kind="ExternalInput").ap()
    out = nc.dram_tensor("out", [2, S, D], F32, kind="ExternalOutput").ap()
    vscr = nc.dram_tensor("vscr", [8, UW], BF16)
    hscr = nc.dram_tensor("hscr", [2, S, D], F32, kind="ExternalOutput" if debug else "Internal").ap()
    if debug:
        dbg = {
            "nT": nc.dram_tensor("dbg_nT", [128, 8 * S], BF16, kind="ExternalOutput").ap(),
            "QAT": nc.dram_tensor("dbg_QAT", [128, 4 * S], BF16, kind="ExternalOutput").ap(),
            "KAT": nc.dram_tensor("dbg_KAT", [128, S], BF16, kind="ExternalOutput").ap(),
            "QBT": nc.dram_tensor("dbg_QBT", [128, 4 * S], BF16, kind="ExternalOutput").ap(),
            "KBT": nc.dram_tensor("dbg_KBT", [128, 4 * S], BF16, kind="ExternalOutput").ap(),
            "VA": nc.dram_tensor("dbg_VA", [128, NT * 2 * 65], BF16, kind="ExternalOutput").ap(),
            "VB": nc.dram_tensor("dbg_VB", [128, NT * 8 * 65], BF16, kind="ExternalOutput").ap(),
            "Y": nc.dram_tensor("dbg_Y", [128, NT * 1024], BF16, kind="ExternalOutput").ap(),
            "ssqh": nc.dram_tensor("dbg_ssqh", [128, 256], F32, kind="ExternalOutput").ap(),
            "Hk": nc.dram_tensor("dbg_Hk", [128, HW], BF16, kind="ExternalOutput").ap(),
            "sm": nc.dram_tensor("dbg_sm", [128, 512], F32, kind="ExternalOutput").ap(),
        }
        d_dbg = None

    ARENA = 212800
    arena = nc.alloc_sbuf_tensor("arena", [128, ARENA], U8)

    def view(off, n, dt, pat=None, **kw):
        esz = 4 if dt == F32 else 2
        assert off % 32 == 0 and off + n * esz <= ARENA, (off, n)
        v = arena[:, off:off + n * esz].bitcast(dt)
        if pat:
            v = v.rearrange(pat, **kw)
        return v

    class Alloc:
        def __init__(self, base):
            self.off = base

        def take(self, nbytes):
            o = self.off
            self.off += (nbytes + 31) // 32 * 32
            return o

    al = Alloc(0)
    cb = view(al.take(1024), 512, BF16, "p (a b) -> p a b", a=4)
    ident, Jm, onesblk, Rmat = cb[:, 0, :], cb[:, 1, :], cb[:, 2, :], cb[:, 3, :]
    sm = view(al.take(2048), 512, F32)
    GIN, GOUT, GMLP, GQ, GK, EPSC, GQ8 = 0, 8, 16, 24, 25, 26, 27
    MLP_BASE = al.off
    Win = view(al.take(8 * 2304 * 2), 8 * 2304, BF16, "p (c n) -> p c n", c=8)
    QAT = view(al.take(4 * S * 2), 4 * S, BF16, "p (c n) -> p c n", c=4)
    KAT = view(al.take(S * 2), S, BF16)
    QBT = view(al.take(4 * S * 2), 4 * S, BF16, "p (c n) -> p c n", c=4)
    KBT = view(al.take(4 * S * 2), 4 * S, BF16, "p (c n) -> p c n", c=4)
    VA = view(al.take(NT * 2 * 65 * 2), NT * 2 * 65, BF16, "p (t h e) -> p t h e", t=NT, h=2)
    VB = view(al.take(NT * 8 * 65 * 2), NT * 8 * 65, BF16, "p (t h e) -> p t h e", t=NT, h=8)
    WOUT_OFF = al.off
    Wout = view(al.take(8 * 1024 * 2), 8 * 1024, BF16, "p (c n) -> p c n", c=8)
    R0 = al.off
    ar = Alloc(R0)
    nT = view(ar.take(8 * S * 2), 8 * S, BF16, "p (c n) -> p c n", c=8)
    ropeCb = [view(ar.take(2048), 512, F32) for _ in range(2)]
    ropeSb = [view(ar.take(2048), 512, F32) for _ in range(2)]
    NXB = 3
    NXS = 4
    xs = [view(ar.take(4096), 1024, F32) for _ in range(NXS)]
    xnb = [view(ar.take(2048), 1024, BF16) for _ in range(NXB)]
    sqb = [view(ar.take(1024), 512, BF16) for _ in range(2)]
    rtb = [view(ar.take(2048), 512, F32) for _ in range(2)]
    rinv = rtb
    qnb = [view(ar.take(1024), 512, BF16) for _ in range(2)]
    t1b = [view(ar.take(2048), 512, F32) for _ in range(2)]
    t2b = [view(ar.take(2048), 512, F32) for _ in range(2)]
    assert ar.off <= ARENA, ar.off
    aa = Alloc(R0)
    Hk = [view(aa.take(HW * 2), HW, BF16) for _ in range(4)]
    Y = view(aa.take(NT * 1024 * 2), NT * 1024, BF16, "p (t n) -> p t n", t=NT)
    NP = 4
    Pb = [view(aa.take(2048), 1024, BF16) for _ in range(NP)]
    yfb = [view(aa.take(1024), 256, F32, "p (a b) -> p a b", a=4) for _ in range(2)]
    ysq = view(aa.take(1024), 256, F32, "p (a b) -> p a b", a=4)
    rdenb = [view(aa.take(32), 4, F32) for _ in range(2)]
    ocp = [view(aa.take(1056), 260, F32, "p (a b) -> p a b", b=65) for _ in range(2)]
    ssqh = view(aa.take(2 * 8 * 16 * 4), 2 * 8 * 16, F32, "p (k h t) -> p k h t", k=2, h=8)
    assert aa.off <= ARENA, aa.off
    Y_off_end = R0 + 4 * ((HW * 2 + 31) // 32 * 32) + NT * 1024 * 2
    ap_ = Alloc(R0)
    mixT = [view(ap_.take(2048), 1024, BF16, "p (c n) -> p c n", c=8) for _ in range(2)]
    xr = [view(ap_.take(4096), 1024, F32) for _ in range(2)]
    hst = [view(ap_.take(4096), 1024, F32) for _ in range(2)]
    junkb = view(ap_.take(1024), 512, BF16)
    assert ap_.off <= R0 + 4 * (HW * 2)
    QKV0 = 2048 + 1024 + 8 * 2304 * 2
    as_ = Alloc(QKV0)
    cmat_f = view(as_.take(2048), 512, F32)
    rb33 = view(as_.take(32), 8, F32)
    oh33 = view(as_.take(UW * 4), UW, F32)
    vsb = view(as_.take(UW * 2), UW, BF16)
    as2 = Alloc(QKV0)
    NHB = 4
    hkst = [view(as2.take(HW * 2), HW, BF16) for _ in range(8)]
    tkst = [view(as2.take(HW * 2), HW, BF16) for _ in range(NHB)]
    assert as2.off <= WOUT_OFF
    assert as_.off <= R0
    am = Alloc(MLP_BASE)
    Wup = view(am.take(8 * DFF * 2), 8 * DFF, BF16, "p (c n) -> p c n", c=8)
    Wdn = view(am.take(32 * D * 2), 32 * D, BF16, "p (c n) -> p c n", c=32)
    gfin = view(am.take(4096), 1024, F32)
    G = 256
    aT = view(am.take(32 * G * 2), 32 * G, BF16, "p (c n) -> p c n", c=32)
    hnT = [view(am.take(8 * G * 2), 8 * G, BF16, "p (c n) -> p c n", c=8) for _ in range(2)]
    NHT = 6
    hT = [view(am.take(4096), 1024, F32) for _ in range(NHT)]
    hnb = [view(am.take(2048), 1024, BF16) for _ in range(2)]
    rst = [view(am.take(1024), G, F32) for _ in range(2)]
    ost = [view(am.take(4096), 1024, F32) for _ in range(2)]
    junkm = view(am.take(2048), 1024, BF16)
    assert am.off <= ARENA, am.off

    psall = nc.alloc_psum_tensor("psall", [128, 4096], F32)

    def psb(i):
        return psall[:, i * 512:(i + 1) * 512]

    def psbf(i):
        return psall[:, i * 512:(i + 1) * 512].bitcast(BF16)

    sc = Sched(nc)
    ndma = [0]

    def dsem():
        ndma[0] += 1
        return DmaSem(nc, "d%d" % ndma[0])

    t_small = sc.dma("sync", sm[:, 0:32], small_d, dsem())
    t_cm = sc.dma("sync", cmat_f, cmat_d, dsem())
    t_rb = sc.dma("sync", rb33[0:33, :], rb33_d, dsem())
    t_oh = sc.dma("sync", oh33[0:33, :], oh33_d, dsem())
    t_cb = sc.op("vector", lambda e: e.tensor_copy(out=cb.rearrange("p a b -> p (a b)"), in_=cmat_f), [t_cm])
    t_eps = sc.op("vector", lambda e: e.memset(sm[:, EPSC:EPSC + 1], EPS), [t_small])
    t_gq8 = sc.op("vector", lambda e: e.tensor_scalar_mul(out=sm[:, GQ8:GQ8 + 1], in0=sm[:, GQ:GQ + 1], scalar1=0.125), [t_small])
    t_v = []
    for j in range(UW // 512):
        tm = sc.op("tensor", lambda e, j=j: e.matmul(psb(j)[0:8, :], lhsT=rb33[0:33, :], rhs=oh33[0:33, j * 512:(j + 1) * 512],
                                                     start=True, stop=True), [t_rb, t_oh])
        t_v.append(sc.op("vector", lambda e, j=j: e.tensor_copy(out=vsb[0:8, j * 512:(j + 1) * 512], in_=psb(j)[0:8, :]), [tm]))
    d_v = dsem()
    t_vscr = sc.dma("sync", vscr.ap(), vsb[0:8, :], d_v, t_v)
    ps_free = [[t_v[j]] if j < UW // 512 else [] for j in range(8)]
    d_win = dsem()
    t_win = None
    for c in range(8):
        t_win = sc.dma("gpsimd", Win[:, c, :], w_in[c * 128:(c + 1) * 128, :], d_win)
    t_win = [t_win]
    d_wout = dsem()
    t_wout = None
    for c in range(8):
        t_wout = sc.dma("gpsimd", Wout[:, c, :], w_out[c * 128:(c + 1) * 128, :], d_wout)
    t_wout = [t_wout]
    setup_done = t_win + t_wout + [t_vscr, t_cb, t_eps, t_gq8] + t_v

    d_x = [dsem() for _ in range(4)]
    d_ropeC = [dsem() for _ in range(2)]
    d_ropeS = [dsem() for _ in range(2)]
    d_hk = [dsem() for _ in range(4)]
    d_xr = [dsem() for _ in range(2)]
    d_hs = [dsem() for _ in range(2)]

    region_free = list(setup_done)
    SX, RX, TX = 128, 160, 192
    SA, RA, TA = 224, 256, 288

    for s in range(2):
        psr = Ring([3, 4, 5, 6, 7])
        psr.free = [list(ps_free[b]) for b in psr.bufs]
        psr_p = Ring([0, 1])
        psr_p.free = [list(ps_free[b]) for b in psr_p.bufs]
        psr_s = Ring([2])
        psr_s.free = [list(ps_free[b]) for b in psr_s.bufs]
        xfree = [list(region_free) for _ in range(NXS)]
        xnfree = [list(region_free) for _ in range(NXB)]
        t_nT = [None] * NT
        qk_tokens, a_tokens = [], []
        t_vones = sc.op("gpsimd", lambda e: e.memset(VA[:, :, :, 64:65], 1.0), [region_free])
        t_vones2 = sc.op("gpsimd", lambda e: e.memset(VB[:, :, :, 64:65], 1.0), [region_free])
        v_tokens = [t_vones, t_vones2]
        ropefree = [list(region_free), list(region_free)]
        rope_tok = {}
        rope_users = {0: [], 1: [], 2: [], 3: []}
        ab_free = {"sq": [[], []], "rt": [[], []], "ri": [[], []], "qn": [[], []], "t1": [[], []], "t2": [[], []]}
        acount = [0]

        def tg_dep(tg):
            r = t_nT[tg * 4:(tg + 1) * 4]
            assert all(t is not None for t in r), tg
            return r

        def tile_unit(i):
            b = i % NXB
            bx = i % NXS
            tl = sc.dma("sync", xs[bx], x[s, i * 128:(i + 1) * 128, :], d_x[bx], xfree[bx])
            ta = sc.op("scalar", lambda e: e.activation(out=xnb[b], in_=xs[bx], func=AF.Square,
                                                        accum_out=sm[:, SX + i:SX + i + 1]), [tl, xnfree[b]])
            tb_ = sc.op("scalar", lambda e: e.activation(out=sm[:, TX + i:TX + i + 1], in_=sm[:, SX + i:SX + i + 1], func=AF.Ln,
                                                         bias=sm[:, EPSC:EPSC + 1], scale=1.0 / D), [ta, t_eps])
            tr = sc.op("scalar", lambda e: e.activation(out=sm[:, RX + i:RX + i + 1], in_=sm[:, TX + i:TX + i + 1], func=AF.Exp,
                                                        scale=-0.5), [tb_])
            tn = sc.op("vector", lambda e: e.tensor_scalar_mul(out=xnb[b], in0=xs[bx], scalar1=sm[:, RX + i:RX + i + 1]), [tr, ta])
            xfree[bx] = [tn]
            yield
            bi, bank, fr = psr.get()
            tt = None
            for c in range(8):
                tt = sc.op("tensor", lambda e, c=c: e.transpose(psbf(bank)[:, c * 128:(c + 1) * 128],
                                                                xnb[b][:, c * 128:(c + 1) * 128], ident),
                           [tn, fr, t_cb, region_free], sig=(c == 7))
            xnfree[b] = [tt]
            te = sc.op("vector", lambda e: e.tensor_tensor(
                out=nT[:, :, i * 128:(i + 1) * 128], in0=psbf(bank).rearrange("p (c n) -> p c n", c=8),
                in1=sm[:, GIN:GIN + 8].unsqueeze(2).to_broadcast([128, 8, 128]), op=ALU.mult), [tt, t_small])
            psr.release(bi, te)
            t_nT[i] = te
            yield

        def qkb_unit(j, tg):
            isq = j < 4
            col0 = 640 + j * 128 if isq else 1152 + (j - 4) * 128
            bi, bank, fr = psr.get()
            tm = None
            for c in range(8):
                tm = sc.op("tensor", lambda e, c=c: e.matmul(
                    psb(bank), lhsT=Win[:, c, col0:col0 + 128], rhs=nT[:, c, tg * 512:(tg + 1) * 512],
                    start=(c == 0), stop=(c == 7)), [tg_dep(tg), fr, t_win, region_free], sig=(c == 7))
            dst = (QBT if isq else KBT)[:, j % 4, tg * 512:(tg + 1) * 512]
            if (j + tg) % 2 == 0:
                te = sc.op("vector", lambda e: e.tensor_scalar_mul(out=dst, in0=psb(bank), scalar1=0.125 if isq else 1.0), [tm])
            else:
                te = sc.op("scalar", lambda e: e.activation(out=dst, in_=psb(bank), func=AF.Copy, scale=0.125 if isq else 1.0), [tm])
            psr.release(bi, te)
            qk_tokens.append(te)
            yield

        def v_unit(i, which):
            assert t_nT[i] is not None
            bi, bank, fr = psr.get()
            tm = None
            if which == "B":
                for c in range(8):
                    tm = sc.op("tensor", lambda e, c=c: e.matmul(
                        psb(bank), lhsT=nT[:, c, i * 128:(i + 1) * 128], rhs=Win[:, c, 1792:2304],
                        start=(c == 0), stop=(c == 7)), [t_nT[i], fr, t_win, region_free], sig=(c == 7))
                src = psb(bank).rearrange("p (h d) -> p h d", h=8)
                dst = VB[:, i, :, 0:64]
                tv = t_vones2
            else:
                for c in range(8):
                    tm = sc.op("tensor", lambda e, c=c: e.matmul(
                        psb(bank)[:, 0:128], lhsT=nT[:, c, i * 128:(i + 1) * 128], rhs=Win[:, c, 1664:1792],
                        start=(c == 0), stop=(c == 7)), [t_nT[i], fr, t_win, region_free], sig=(c == 7))
                src = psb(bank)[:, 0:128].rearrange("p (h d) -> p h d", h=2)
                dst = VA[:, i, :, 0:64]
                tv = t_vones
            if i % 2 == 0:
                te = sc.op("vector", lambda e: e.tensor_copy(out=dst, in_=src), [tm, tv])
            else:
                te = sc.op("scalar", lambda e: e.activation(out=dst, in_=src, func=AF.Copy), [tm, tv])
            psr.release(bi, te)
            v_tokens.append(te)
            yield

        def a_unit(j, tg):
            if tg not in rope_tok:
                rbk_ = tg % 2
                rope_tok[tg] = (sc.dma("sync", ropeCb[rbk_], ropeC_d[:, tg * 512:(tg + 1) * 512], d_ropeC[rbk_], ropefree[rbk_]),
                                sc.dma("sync", ropeSb[rbk_], ropeS_d[:, tg * 512:(tg + 1) * 512], d_ropeS[rbk_], ropefree[rbk_]))
            rbk = tg % 2
            tlc, tls = rope_tok[tg]
            col0 = j * 128 if j < 4 else 512
            gcol = GQ8 if j < 4 else GK
            dst = QAT[:, j, tg * 512:(tg + 1) * 512] if j < 4 else KAT[:, tg * 512:(tg + 1) * 512]
            k = acount[0] % 2
            acount[0] += 1
            bi, bank, fr = psr_p.get()
            tm = None
            for c in range(8):
                tm = sc.op("tensor", lambda e, c=c: e.matmul(
                    psb(bank), lhsT=Win[:, c, col0:col0 + 128], rhs=nT[:, c, tg * 512:(tg + 1) * 512],
                    start=(c == 0), stop=(c == 7)), [tg_dep(tg), fr, t_win, region_free], sig=(c == 7))
            tsq = sc.op("scalar", lambda e: e.activation(out=sqb[k], in_=psb(bank), func=AF.Square), [tm, ab_free["sq"][k]])
            yield
            b2i, bank2, fr2 = psr_s.get()
            tss = sc.op("tensor", lambda e: e.matmul(psb(bank2), lhsT=onesblk, rhs=sqb[k], start=True, stop=True), [tsq, fr2, t_cb])
            ab_free["sq"][k] = [tss]
            trt = sc.op("scalar", lambda e: e.activation(out=rtb[k], in_=psb(bank2), func=AF.Ln,
                                                         bias=sm[:, EPSC:EPSC + 1], scale=1.0 / 64), [tss, ab_free["rt"][k], ab_free["ri"][k], t_eps])
            psr_s.release(b2i, trt)
            tri = sc.op("scalar", lambda e: e.activation(out=rinv[k], in_=rtb[k], func=AF.Exp, scale=-0.5), [trt, ab_free["ri"][k]])
            ab_free["rt"][k] = [tri]
            tqn = sc.op("vector", lambda e: e.scalar_tensor_tensor(
                out=qnb[k], in0=psb(bank), scalar=sm[:, gcol:gcol + 1], in1=rinv[k], op0=ALU.mult, op1=ALU.mult),
                [tri, tm, ab_free["qn"][k], t_gq8])
            ab_free["ri"][k] = [tqn]
            psr_p.release(bi, tqn)
            yield
            b3i, bank3, fr3 = psr_s.get()
            trot = sc.op("tensor", lambda e: e.matmul(psb(bank3), lhsT=Rmat, rhs=qnb[k], start=True, stop=True), [tqn, fr3, t_cb])
            tt1 = sc.op("gpsimd", lambda e: e.tensor_tensor(out=t1b[k], in0=qnb[k], in1=ropeCb[rbk], op=ALU.mult),
                        [tqn, tlc, ab_free["t1"][k]])
            tt2 = sc.op("vector", lambda e: e.tensor_tensor(out=t2b[k], in0=psb(bank3), in1=ropeSb[rbk], op=ALU.mult),
                        [trot, tls, ab_free["t2"][k]])
            psr_s.release(b3i, tt2)
            ab_free["qn"][k] = [trot, tt1]
            tfin = sc.op("gpsimd", lambda e: e.tensor_tensor(out=dst, in0=t1b[k], in1=t2b[k], op=ALU.add), [tt1, tt2])
            ab_free["t1"][k] = [tfin]
            ab_free["t2"][k] = [tfin]
            a_tokens.append(tfin)
            rope_users[tg].extend([tt1, tt2])
            if len(rope_users[tg]) == 10:
                ropefree[rbk] = list(rope_users[tg])
            yield

        def rdy_tg(tg):
            return lambda: all(t is not None for t in t_nT[4 * tg:4 * tg + 4])

        def rdy_tile(i):
            return lambda: t_nT[i] is not None

        def rdy_xbuf(i):
            return lambda: i < NXB or t_nT[i - NXB] is not None

        pending = [(tile_unit(i), False, rdy_xbuf(i)) for i in range(4)]
        for tg in range(4):
            nxt_tiles = [(tile_unit(i), False, rdy_xbuf(i)) for i in range(4 * (tg + 1), 4 * (tg + 2))] if tg < 3 else []
            items = []
            for j in range(4):
                items += [(a_unit(j, tg), True, rdy_tg(tg)), (qkb_unit(j, tg), False, rdy_tg(tg)), (qkb_unit(4 + j, tg), False, rdy_tg(tg))]
                if j < len(nxt_tiles):
                    items.append(nxt_tiles[j])
                items += [(v_unit(4 * tg + j, "B"), False, rdy_tile(4 * tg + j))]
            items += [(a_unit(4, tg), True, rdy_tg(tg))] + [(v_unit(4 * tg + j, "A"), False, rdy_tile(4 * tg + j)) for j in range(4)]
            pending += items
        active = []
        WIN = 9
        while pending or active:
            while pending and len(active) < WIN:
                nA = sum(1 for a_ in active if a_[1])
                pick = None
                for idx_, it_ in enumerate(pending):
                    if it_[2]() and not (it_[1] and nA >= 2):
                        pick = idx_
                        break
                    if idx_ > 12:
                        break
                if pick is None:
                    break
                active.append(pending.pop(pick))
            assert active, "unit scheduler stuck"
            for a_ in list(active):
                try:
                    next(a_[0])
                except StopIteration:
                    active.remove(a_)
        proj_done = qk_tokens + v_tokens + a_tokens
        if debug and s == 0:
            d_dbg = dsem()
            fl = lambda v, p: v.rearrange(p)
            dt_ = [sc.dma("sync", dbg["nT"], fl(nT, "p c n -> p (c n)"), dsem(), proj_done),
                   sc.dma("sync", dbg["QAT"], fl(QAT, "p c n -> p (c n)"), dsem(), proj_done),
                   sc.dma("sync", dbg["KAT"], KAT, dsem(), proj_done),
                   sc.dma("sync", dbg["QBT"], fl(QBT, "p c n -> p (c n)"), dsem(), proj_done),
                   sc.dma("sync", dbg["KBT"], fl(KBT, "p c n -> p (c n)"), dsem(), proj_done),
                   sc.dma("sync", dbg["VA"], fl(VA, "p t h e -> p (t h e)"), dsem(), proj_done),
                   sc.dma("sync", dbg["VB"], fl(VB, "p t h e -> p (t h e)"), dsem(), proj_done)]
            proj_done = proj_done + dt_
        ps_free = [[] for _ in range(8)]
        for rg in (psr, psr_p, psr_s):
            for bidx, fr_ in zip(rg.bufs, rg.free):
                ps_free[bidx] = list(fr_)
        if s == 1:
            d_up_early = [dsem() for _ in range(4)]
            early_up = {}
            for q4 in range(4):
                tk = None
                for c in range(4):
                    tk = sc.dma("gpsimd", Wup[:, c, q4 * 1024:(q4 + 1) * 1024],
                                w_up[c * 128:(c + 1) * 128, q4 * 1024:(q4 + 1) * 1024], d_up_early[q4], proj_done)
                early_up[q4] = tk

        NSB = 3
        S_free = [list(ps_free[2 * b]) + list(ps_free[2 * b + 1]) for b in range(NSB)]
        O_free = [list(ps_free[6]), list(ps_free[7])]
        P_free = [list(proj_done) for _ in range(NP)]
        hk_free = [list(proj_done) for _ in range(4)]
        yf_free = [list(proj_done), list(proj_done)]
        ysq_free = [list(proj_done)]
        rden_free = [list(proj_done), list(proj_done)]
        ocp_free = [list(proj_done), list(proj_done)]
        y_tokens = []
        ssq_tokens = []
        hk_dma = {}
        hk_tok = {}
        obank = [6, 7]

        def load_hk(pair):
            for half in range(2):
                slot = (2 * pair + half) % 4
                h = 2 * pair + half
                hk_tok[(pair, half)] = sc.dma("sync", Hk[slot], bass.AP(vscr, h * UW, [[1, 128], [1, HW]]), d_hk[slot],
                                              [hk_free[slot], t_vscr])

        steps = []
        for kind in ("A", "B"):
            for pair in range(4):
                for qc in range(4):
                    kts = list(range(16)) if kind == "A" else list(range(max(0, 4 * qc - 8), min(15, 4 * qc + 11) + 1))
                    for n, kt in enumerate(kts):
                        steps.append(dict(kind=kind, pair=pair, qc=qc, kt=kt, first=(n == 0), last=(n == len(kts) - 1)))

        def KT(st, half):
            kt = st["kt"]
            if st["kind"] == "A":
                return KAT[half * 64:(half + 1) * 64, kt * 128:(kt + 1) * 128]
            return KBT[half * 64:(half + 1) * 64, st["pair"], kt * 128:(kt + 1) * 128]

        def qrange(st):
            if st["kind"] == "A":
                return 0, 4
            qbs = [qb for qb in range(4) if abs(st["kt"] - 4 * st["qc"] - qb) <= 8]
            assert qbs and qbs == list(range(qbs[0], qbs[-1] + 1))
            return qbs[0], qbs[-1] + 1

        def QT(st, half):
            qc = st["qc"]
            lo, hi = qrange(st)
            src = QAT if st["kind"] == "A" else QBT
            return src[half * 64:(half + 1) * 64, st["pair"], qc * 512 + lo * 128:qc * 512 + hi * 128]

        def Vt(st, half):
            if st["kind"] == "A":
                return VA[:, st["kt"], half, :]
            return VB[:, st["kt"], 2 * st["pair"] + half, :]

        qk_tok = {}

        def emit_qk(i):
            st = steps[i]
            buf = i % NSB
            deps = [S_free[buf], proj_done]
            S_free[buf] = []
            t = None
            isB = st["kind"] == "B"
            lo, hi = qrange(st)
            for half in range(2):
                t = sc.op("tensor", lambda e, half=half, buf=buf, st=st, lo=lo, hi=hi: e.matmul(
                    psb(2 * buf + half)[:, lo * 128:hi * 128], lhsT=KT(st, half), rhs=QT(st, half), start=True, stop=(not isB)), deps,
                    sig=(half == 1 and not isB))
            if isB:
                kt, qc, pair = st["kt"], st["qc"], st["pair"]
                c0 = 1408 - (kt * 128 - qc * 512)
                assert 0 <= c0 and c0 + 512 <= HW
                for half in range(2):
                    slot = (2 * pair + half) % 4
                    t = sc.op("tensor", lambda e, half=half, buf=buf, slot=slot, c0=c0, lo=lo, hi=hi: e.matmul(
                        psb(2 * buf + half)[:, lo * 128:hi * 128], lhsT=Jm, rhs=Hk[slot][:, c0 + lo * 128:c0 + hi * 128],
                        start=False, stop=True), [hk_tok[(pair, half)]], sig=(half == 1))
                st["mul"] = t
            qk_tok[i] = t

        def epilogue(st):
            kind, pair, qc = st["kind"], st["pair"], st["qc"]
            heads = [pair, pair + 4] if kind == "A" else [2 * pair, 2 * pair + 1]
            tcps = []
            for half in range(2):
                O3 = psb(obank[half])[:, 0:260].rearrange("p (a b) -> p a b", b=65)
                tcp = sc.op("vector", lambda e, half=half, O3=O3: e.tensor_copy(out=ocp[half], in_=O3), [st["pv"][half], ocp_free[half]])
                O_free[half] = [tcp]
                tcps.append(tcp)
            trds = []
            for half in range(2):
                trds.append(sc.op("vector", lambda e, half=half: e.reciprocal(out=rdenb[half], in_=ocp[half][:, :, 64]),
                                  [tcps[half], rden_free[half]]))
            for half in range(2):
                h = heads[half]
                colb = h * 64 + (0 if kind == "A" else 512)
                ty = sc.op("gpsimd", lambda e, half=half, colb=colb, qc=qc: e.tensor_tensor(
                    out=Y[:, qc * 4:(qc + 1) * 4, colb:colb + 64], in0=ocp[half][:, :, 0:64],
                    in1=rdenb[half].unsqueeze(2).to_broadcast([128, 4, 64]), op=ALU.mult), [trds[half], proj_done])
                ocp_free[half] = [ty]
                rden_free[half] = [ty]
                y_tokens.append(ty)

        load_hk(0)
        load_hk(1)
        LOOK = 2
        for i in range(min(LOOK, len(steps))):
            emit_qk(i)
        for i, st in enumerate(steps):
            if i + LOOK < len(steps):
                emit_qk(i + LOOK)
            buf = i % NSB
            pi = i % NP
            tex = sc.op("scalar", lambda e, buf=buf, pi=pi: e.activation(
                out=Pb[pi], in_=psall[:, buf * 1024:(buf + 1) * 1024], func=AF.Exp), [qk_tok[i], P_free[pi]])
            S_free[buf] = [tex]
            tready = tex
            tpv = None
            st["pv"] = [None, None]
            plo, phi = qrange(st)
            if st["first"]:
                assert plo == 0
            for half in range(2):
                for qb in range(plo, phi):
                    deps = [tready]
                    if st["first"]:
                        deps.append(O_free[half])
                    tpv = sc.op("tensor", lambda e, half=half, pi=pi, qb=qb, st=st: e.matmul(
                        psb(obank[half])[:, qb * 65:(qb + 1) * 65], lhsT=Pb[pi][:, half * 512 + qb * 128:half * 512 + (qb + 1) * 128],
                        rhs=Vt(st, half), start=(st["first"] and qb == 0), stop=st["last"], skip_group_check=True),
                        deps, sig=(qb == phi - 1))
                if st["first"]:
                    O_free[half] = []
                st["pv"][half] = tpv
            P_free[pi] = [tpv]
            if st["last"]:
                epilogue(st)
                if st["kind"] == "B" and st["qc"] == 3:
                    pair = st["pair"]
                    for half in range(2):
                        slot = (2 * pair + half) % 4
                        hk_free[slot] = [st["mul"]]
                    if pair + 2 < 4:
                        load_hk(pair + 2)
        attn_done = y_tokens + [Tok(sc.sem["tensor"], sc.cnt["tensor"], "tensor", sc.idx["tensor"])]
        if debug and s == 0:
            dt_ = [sc.dma("sync", dbg["Y"], Y.rearrange("p t n -> p (t n)"), dsem(), attn_done),
                   sc.dma("sync", dbg["Hk"], Hk[3], dsem(), attn_done)]
            attn_done = attn_done + dt_

        if s == 1:
            FLATE = (WOUT_OFF - (MLP_BASE + 8 * DFF * 2)) // 2048
            d_dn_e = [dsem() for _ in range(4)]
            early_dn = {}
            for f8 in range(4):
                tk = None
                for f in range(f8 * 8, min(f8 * 8 + 8, FLATE)):
                    tk = sc.dma("gpsimd", Wdn[:, f, :], w_dn[f * 128:(f + 1) * 128, :], d_dn_e[f8], attn_done)
                early_dn[f8] = tk
            d_up7 = [dsem() for _ in range(4)]
            early_up7 = {}
            for q4 in range(4):
                tk = None
                for c in range(4, 8):
                    tk = sc.dma("gpsimd", Wup[:, c, q4 * 1024:(q4 + 1) * 1024],
                                w_up[c * 128:(c + 1) * 128, q4 * 1024:(q4 + 1) * 1024], d_up7[q4], attn_done)
                early_up7[q4] = tk

        psr_t = Ring([0, 1])
        psr_t.free = [list(attn_done) for _ in range(2)]
        psr = Ring([2, 3, 4, 5, 6, 7])
        psr.free = [list(attn_done) for _ in range(6)]
        mix_free = [list(attn_done), list(attn_done)]
        xr_free = [list(attn_done), list(attn_done)]
        hst_free = [list(attn_done), list(attn_done)]
        store_toks = []
        tstate = {}

        def p4_T(i):
            b = i % 2
            tlx = sc.dma("sync", xr[b], x[s, i * 128:(i + 1) * 128, :], d_xr[b], xr_free[b])
            tq_ = []
            for ab in range(2):
                tq_.append(sc.op("scalar", lambda e, ab=ab: e.activation(
                    out=junkb, in_=Y[:, i, ab * 512:(ab + 1) * 512], func=AF.Square,
                    accum_out=sm[:, SA + 2 * i + ab:SA + 2 * i + ab + 1]), [y_tokens, attn_done]))
            tln = sc.op("scalar", lambda e: e.activation(out=sm[:, TA + 2 * i:TA + 2 * i + 2], in_=sm[:, SA + 2 * i:SA + 2 * i + 2], func=AF.Ln,
                                                         bias=sm[:, EPSC:EPSC + 1], scale=1.0 / 512), [tq_])
            trs = sc.op("scalar", lambda e: e.activation(out=sm[:, RA + 2 * i:RA + 2 * i + 2], in_=sm[:, TA + 2 * i:TA + 2 * i + 2], func=AF.Exp,
                                                         scale=-0.5), [tln])
            bi, bank, fr = psr_t.get()
            tt = None
            for c in range(8):
                tt = sc.op("tensor", lambda e, c=c: e.transpose(psbf(bank)[:, c * 128:(c + 1) * 128],
                                                                Y[:, i, c * 128:(c + 1) * 128], ident),
                           [y_tokens, fr], sig=(c == 7))
            te = sc.op("vector", lambda e: e.tensor_tensor(
                out=mixT[b], in0=psbf(bank).rearrange("p (c n) -> p c n", c=8),
                in1=sm[:, GOUT:GOUT + 8].unsqueeze(2).to_broadcast([128, 8, 128]), op=ALU.mult), [tt, mix_free[b]])
            psr_t.release(bi, te)
            tstate[i] = (tlx, trs, te)

        def p4_M(i):
            b = i % 2
            tlx, trs, te = tstate[i]
            last_mm = None
            prev = []
            for half in range(2):
                banks = []
                for ab in range(2):
                    bi2, bank2, fr2 = psr.get()
                    tm = None
                    for cc in range(4):
                        c = ab * 4 + cc
                        tm = sc.op("tensor", lambda e, c=c, half=half, bank2=bank2, cc=cc: e.matmul(
                            psb(bank2), lhsT=mixT[b][:, c, :], rhs=Wout[:, c, half * 512:(half + 1) * 512],
                            start=(cc == 0), stop=(cc == 3)), [te, fr2, t_wout], sig=(cc == 3))
                    banks.append((bi2, bank2, tm))
                    last_mm = tm
                (ba_i, ba, tma), (bb_i, bb, tmb) = banks
                hs = hst[b][:, half * 512:(half + 1) * 512]
                th1 = sc.op("vector", lambda e, ba=ba, hs=hs, half=half: e.scalar_tensor_tensor(
                    out=hs, in0=psb(ba), scalar=sm[:, RA + 2 * i:RA + 2 * i + 1], in1=xr[b][:, half * 512:(half + 1) * 512],
                    op0=ALU.mult, op1=ALU.add), [tma, tlx, trs, hst_free[b]])
                psr.release(ba_i, th1)
                th2 = sc.op("vector", lambda e, bb=bb, hs=hs: e.scalar_tensor_tensor(
                    out=hs, in0=psb(bb), scalar=sm[:, RA + 2 * i + 1:RA + 2 * i + 2], in1=hs, op0=ALU.mult, op1=ALU.add),
                    [tmb, th1])
                psr.release(bb_i, th2)
                prev.append(th2)
            mix_free[b] = [last_mm]
            xr_free[b] = list(prev)
            tst = sc.dma("sync", hscr[s, i * 128:(i + 1) * 128, :], hst[b], d_hs[b], prev)
            hst_free[b] = [tst]
            store_toks.append(tst)

        p4_T(0)
        for i in range(NT):
            if i + 1 < NT:
                p4_T(i + 1)
            p4_M(i)
        ps_free = [[] for _ in range(8)]
        for rg in (psr, psr_t):
            for bidx, fr_ in zip(rg.bufs, rg.free):
                ps_free[bidx] = list(fr_)
        region_free = store_toks + [Tok(sc.sem["tensor"], sc.cnt["tensor"], "tensor", sc.idx["tensor"])] + \
            [Tok(sc.sem["vector"], sc.cnt["vector"], "vector", sc.idx["vector"])]

    if debug:
        region_free = region_free + [sc.dma("sync", dbg["sm"], sm, dsem(), region_free)]
    all_done = region_free + [Tok(sc.sem[e], sc.cnt[e], e, sc.idx[e]) for e in ("scalar", "gpsimd")]
    d_g = dsem()
    d_h = [dsem() for _ in range(6)]
    d_o = [dsem() for _ in range(2)]
    t_gfin = sc.dma("sync", gfin, gfin_d.partition_broadcast(128), d_g, all_done)
    sc.wait_all("gpsimd", all_done)
    d_up = [dsem() for _ in range(4)]
    d_dn = [dsem() for _ in range(4)]
    t_wup = [[None] * 4 for _ in range(8)]
    t_wdn = [None] * 32
    for q4 in range(4):
        for c in range(8):
            t_wup[c][q4] = early_up[q4] if c < 4 else early_up7[q4]
    d_dn_late = dsem()
    tk = None
    for f in range(FLATE, 32):
        tk = sc.dma("gpsimd", Wdn[:, f, :], w_dn[f * 128:(f + 1) * 128, :], d_dn_late, all_done)
    for f in range(32):
        t_wdn[f] = early_dn[f // 8] if f < FLATE else tk

    SH, RH, TH = 320, 352, 384
    SO, RO, TO = 416, 448, 480
    psr = Ring(list(range(8)))
    psr.free = [list(all_done) for _ in range(8)]
    h_free = [list(all_done) for _ in range(6)]
    hn_free = [list(all_done), list(all_done)]
    hnT_free = [list(all_done), list(all_done)]
    rst_free = [list(all_done), list(all_done)]
    ost_free = [list(all_done), list(all_done)]
    hcount = [0]
    NG = 2 * S // G
    TPG = G // 128
    final_toks = []

    def load_h(gt):
        s_, i_ = divmod(gt, NT)
        k = hcount[0] % 6
        hcount[0] += 1
        t = sc.dma("sync", hT[k], hscr[s_, i_ * 128:(i_ + 1) * 128, :], d_h[k], h_free[k])
        return k, t

    def prepA(g):
        st_ = []
        for tl_ in range(TPG):
            gt = g * TPG + tl_
            k, tl = load_h(gt)
            b = gt % 2
            ta = sc.op("scalar", lambda e, b=b, k=k, gt=gt: e.activation(out=hnb[b], in_=hT[k], func=AF.Square,
                                                                         accum_out=sm[:, SH + gt:SH + gt + 1]), [tl, hn_free[b]])
            tb_ = sc.op("scalar", lambda e, gt=gt: e.activation(out=sm[:, TH + gt:TH + gt + 1], in_=sm[:, SH + gt:SH + gt + 1], func=AF.Ln,
                                                                bias=sm[:, EPSC:EPSC + 1], scale=1.0 / D), [ta])
            tr = sc.op("scalar", lambda e, gt=gt: e.activation(out=sm[:, RH + gt:RH + gt + 1], in_=sm[:, TH + gt:TH + gt + 1], func=AF.Exp,
                                                               scale=-0.5), [tb_])
            tn = sc.op("vector", lambda e, b=b, k=k, gt=gt: e.tensor_scalar_mul(out=hnb[b], in0=hT[k], scalar1=sm[:, RH + gt:RH + gt + 1]),
                       [tr, ta])
            h_free[k] = [tn]
            st_.append((b, tn))
        return st_

    def prepB(g, st_):
        gb = g % 2
        toks = []
        for tl_, (b, tn) in enumerate(st_):
            bi, bank, fr = psr.get()
            tt = None
            for c in range(8):
                tt = sc.op("tensor", lambda e, c=c, b=b, bank=bank: e.transpose(psbf(bank)[:, c * 128:(c + 1) * 128],
                                                                                hnb[b][:, c * 128:(c + 1) * 128], ident),
                           [tn, fr], sig=(c == 7))
            hn_free[b] = [tt]
            te = sc.op("vector", lambda e, gb=gb, tl_=tl_, bank=bank: e.tensor_tensor(
                out=hnT[gb][:, :, tl_ * 128:(tl_ + 1) * 128], in0=psbf(bank).rearrange("p (c n) -> p c n", c=8),
                in1=sm[:, GMLP:GMLP + 8].unsqueeze(2).to_broadcast([128, 8, 128]), op=ALU.mult), [tt, hnT_free[gb]])
            psr.release(bi, te)
            toks.append(te)
        return toks

    hn_ready = prepB(0, prepA(0))
    for g in range(NG):
        gb = g % 2
        nxtA = None
        up_toks = []
        last_up = None
        for f in range(32):
            if f == 16 and g + 1 < NG:
                nxtA = prepA(g + 1)
            bi, bank, fr = psr.get()
            tm = None
            for c in range(8):
                tm = sc.op("tensor", lambda e, c=c, f=f, gb=gb, bank=bank: e.matmul(
                    psb(bank)[:, 0:G], lhsT=Wup[:, c, f * 128:(f + 1) * 128], rhs=hnT[gb][:, c, :],
                    start=(c == 0), stop=(c == 7)), [hn_ready, fr, t_wup[c][f // 8]], sig=(c == 7))
            k = f % 2
            trl = sc.op("scalar", lambda e, k=k, bank=bank: e.activation(out=rst[k], in_=psb(bank)[:, 0:G], func=AF.Relu), [tm, rst_free[k]])
            psr.release(bi, trl)
            tsq2 = sc.op("gpsimd", lambda e, k=k, f=f: e.tensor_tensor(out=aT[:, f, :], in0=rst[k], in1=rst[k], op=ALU.mult), [trl])
            rst_free[k] = [tsq2]
            up_toks.append(tsq2)
            last_up = tm
        hnT_free[gb] = [last_up]
        nxt_ready = prepB(g + 1, nxtA) if g + 1 < NG else None
        for tl_ in range(TPG):
            gt = g * TPG + tl_
            s_, i_ = divmod(gt, NT)
            k, tlh = load_h(gt)
            ob = gt % 2
            halves = []
            for half in range(2):
                bi, bank, fr = psr.get()
                tm = None
                for f in range(32):
                    tm = sc.op("tensor", lambda e, f=f, tl_=tl_, half=half, bank=bank: e.matmul(
                        psb(bank), lhsT=aT[:, f, tl_ * 128:(tl_ + 1) * 128], rhs=Wdn[:, f, half * 512:(half + 1) * 512],
                        start=(f == 0), stop=(f == 31)), [up_toks[f], fr, t_wdn[f]], sig=(f == 31))
                to = sc.op("vector", lambda e, ob=ob, k=k, half=half, bank=bank: e.tensor_tensor(
                    out=ost[ob][:, half * 512:(half + 1) * 512], in0=psb(bank), in1=hT[k][:, half * 512:(half + 1) * 512], op=ALU.add),
                    [tm, tlh, ost_free[ob]])
                psr.release(bi, to)
                halves.append(to)
            h_free[k] = list(halves)
            ta = sc.op("scalar", lambda e, ob=ob, gt=gt: e.activation(out=junkm, in_=ost[ob], func=AF.Square,
                                                                      accum_out=sm[:, SO + gt:SO + gt + 1]), [halves, all_done])
            tb_ = sc.op("scalar", lambda e, gt=gt: e.activation(out=sm[:, TO + gt:TO + gt + 1], in_=sm[:, SO + gt:SO + gt + 1], func=AF.Ln,
                                                                bias=sm[:, EPSC:EPSC + 1], scale=1.0 / D), [ta])
            tr = sc.op("scalar", lambda e, gt=gt: e.activation(out=sm[:, RO + gt:RO + gt + 1], in_=sm[:, TO + gt:TO + gt + 1], func=AF.Exp,
                                                               scale=-0.5), [tb_])
            tf = sc.op("vector", lambda e, ob=ob, gt=gt: e.scalar_tensor_tensor(
                out=ost[ob], in0=ost[ob], scalar=sm[:, RO + gt:RO + gt + 1], in1=gfin, op0=ALU.mult, op1=ALU.mult), [tr, t_gfin, halves])
            tst = sc.dma("sync", out[s_, i_ * 128:(i_ + 1) * 128, :], ost[ob], d_o[ob], [tf])
            ost_free[ob] = [tst]
            final_toks.append(tst)
        hn_ready = nxt_ready

    sc.wait_all("sync", final_toks)
    sc.flush()
    return nc


def _t5_bucket(rel):
    rel = np.asarray(rel, np.int64)
    nb_half, max_exact = 16, 8
    ret = np.where(rel > 0, nb_half, 0)
    n = np.abs(rel)
    lg = (np.log(np.maximum(n, 1).astype(np.float32) / np.float32(max_exact)) / np.float32(math.log(1024 / max_exact))
          * np.float32(nb_half - max_exact)).astype(np.int32)
    large = np.minimum(max_exact + lg, nb_half - 1)
    return ret + np.where(n < max_exact, n, large)


def _constants():
    ident = np.eye(128, dtype=np.float32)
    J = ident[::-1].copy()
    ob = np.zeros((128, 128), np.float32)
    ob[:64, :64] = 1
    ob[64:, 64:] = 1
    R = np.zeros((128, 128), np.float32)
    for m in range(128):
        d = m % 64
        base = m - d
        blk = d // 32
        dd = d % 32
        if dd < 16:
            R[base + blk * 32 + dd + 16, m] = -1.0
        else:
            R[base + blk * 32 + dd - 16, m] = 1.0
    cmat = np.concatenate([ident, J, ob, R], axis=1).astype(np.float32)
    t = np.arange(S)
    row = (t // 64).astype(np.float64)
    col = (t % 64).astype(np.float64)
    inv = (10000.0 ** (-np.arange(16, dtype=np.float32) / np.float32(16))).astype(np.float32).astype(np.float64)
    C = np.zeros((128, S), np.float32)
    Sn = np.zeros((128, S), np.float32)
    for p in range(128):
        d = p % 64
        pos = row if d < 32 else col
        f = (d % 32) % 16
        ang = (pos.astype(np.float32) * np.float32(inv[f])).astype(np.float32)
        C[p] = np.cos(ang.astype(np.float64))
        Sn[p] = np.sin(ang.astype(np.float64))
    u = np.arange(UW)
    d = 1535 - u
    ad = np.abs(d)
    mult = (ad <= 64).astype(np.int64) + ((d % 4 == 0) & (ad <= 256)) + ((d % 16 == 0) & (ad <= 1024))
    mult[u >= 3071] = 0
    bkt = _t5_bucket(d)
    oh = np.zeros((33, UW), np.float32)
    valid = mult > 0
    oh[bkt[valid], u[valid]] = 1.0
    oh[32] = np.where(valid, np.log(np.maximum(mult, 1)), -30000.0)
    return cmat, C, Sn, oh


_CACHE = {}


def kernel(x, attn_norm_g, w_in, q_norm_g, k_norm_g, rel_bias, out_norm_a_g, out_norm_b_g,
           w_out, mlp_norm_g, w_up, w_down, final_norm_g):
    x = np.asarray(x, np.float32)
    f = lambda a: np.ascontiguousarray(np.asarray(a, np.float32))
    w_in0 = f(w_in)[0]
    cols = []
    for i in range(4):
        cols += list(range(64 * i, 64 * i + 64)) + list(range(64 * (i + 4), 64 * (i + 4) + 64))
    cols += list(range(512, 640))
    cols += list(range(768, 1280))
    cols += list(range(1280, 1792))
    cols += list(range(640, 768))
    cols += list(range(1792, 2304))
    w_in_p = np.ascontiguousarray(w_in0[:, np.array(cols)])
    small = np.zeros((128, 32), np.float32)
    small[:, 0:8] = f(attn_norm_g)[0].reshape(8, 128).T
    small[:, 8:16] = np.concatenate([f(out_norm_a_g)[0], f(out_norm_b_g)[0]]).reshape(8, 128).T
    small[:, 16:24] = f(mlp_norm_g)[0].reshape(8, 128).T
    small[:, 24] = np.tile(f(q_norm_g)[0], 2)
    small[:, 25] = np.tile(f(k_norm_g)[0], 2)
    rb33 = np.concatenate([f(rel_bias), np.ones((1, 8), np.float32)], axis=0)
    if "c" not in _CACHE:
        _CACHE["c"] = _constants()
    cmat, C, Sn, oh = _CACHE["c"]
    nc = build_nc()
    shared = {
        "w_in": w_in_p, "w_out": f(w_out)[0], "w_up": f(w_up)[0], "w_dn": f(w_down)[0],
        "small": small, "gfin": f(final_norm_g).reshape(1, D), "cmat": cmat, "ropeC": C, "ropeS": Sn,
        "rb33": rb33, "oh33": oh,
    }
    in_maps = []
    for c in range(NCORES):
        m = dict(shared)
        m["x"] = np.ascontiguousarray(x[2 * c:2 * c + 2])
        in_maps.append(m)
    res = run_bass_kernel_spmd(nc, in_maps, core_ids=list(range(NCORES)))
    return np.concatenate([r["out"] for r in res.results], axis=0).astype(np.float32)
```

```python
import math
import numpy as np
import concourse.bass as bass
import concourse.mybir as mybir
from concourse.bass_utils import run_bass_kernel_spmd

F32 = mybir.dt.float32
BF16 = mybir.dt.bfloat16
U8 = mybir.dt.uint8
AF = mybir.ActivationFunctionType
ALU = mybir.AluOpType
AX = mybir.AxisListType

NCORES = 8
S = 2048
D = 1024
NT = 16
DFF = 4096
EPS = 1e-6
HW = 2944
UW = 3072
ENGS = ["sync", "scalar", "gpsimd", "vector", "tensor"]


class Tok:
    __slots__ = ("sem", "val", "eng", "idx")

    def __init__(self, sem, val, eng, idx):
        self.sem, self.val, self.eng, self.idx = sem, val, eng, idx


class DmaSem:
    def __init__(self, nc, name):
        self.h = nc.alloc_semaphore(name)
        self.val = 0


class Sched:
    def __init__(self, nc):
        self.nc = nc
        self.q = {e: [] for e in ENGS}
        self.sem = {e: nc.alloc_semaphore("c_" + e) for e in ENGS}
        self.cnt = {e: 0 for e in ENGS}
        self.idx = {e: 0 for e in ENGS}
        self.waited = {e: {} for e in ENGS}

    def _flat(self, deps, acc):
        for t in deps:
            if t is None:
                continue
            if isinstance(t, (list, tuple)):
                self._flat(t, acc)
            else:
                k = t.sem.num
                if k not in acc or acc[k].val < t.val:
                    acc[k] = t

    def _waits(self, eng, deps):
        acc = {}
        self._flat(deps, acc)
        for key, t in acc.items():
            if self.waited[eng].get(key, 0) >= t.val:
                continue
            if t.eng == eng and t.idx is not None and self.idx[eng] - t.idx >= 3:
                continue
            self.waited[eng][key] = t.val
            self.q[eng].append(lambda e, s=t.sem, v=t.val: e.wait_ge(s, v))

    def op(self, eng, fn, deps=(), sig=True):
        self._waits(eng, deps)
        i = self.idx[eng]
        self.idx[eng] += 1
        if sig:
            self.cnt[eng] += 1
            sem, val = self.sem[eng], self.cnt[eng]
            self.q[eng].append(lambda e: fn(e).then_inc(sem, 1))
            return Tok(sem, val, eng, i)
        self.q[eng].append(lambda e: fn(e))
        return None

    def dma(self, eng, out, in_, dsem, deps=()):
        self._waits(eng, deps)
        self.idx[eng] += 1
        dsem.val += 16
        h = dsem.h
        self.q[eng].append(lambda e: e.dma_start(out=out, in_=in_).then_inc(h, 16))
        return Tok(h, dsem.val, "dma", None)

    def wait_all(self, eng, deps):
        self._waits(eng, deps)

    def flush(self):
        nc = self.nc
        qs = self.q
        with nc.Block() as block:
            @block.sync
            def _(e):
                for f in qs["sync"]:
                    f(e)

            @block.scalar
            def _(e):
                for f in qs["scalar"]:
                    f(e)

            @block.gpsimd
            def _(e):
                for f in qs["gpsimd"]:
                    f(e)

            @block.vector
            def _(e):
                for f in qs["vector"]:
                    f(e)

            @block.tensor
            def _(e):
                for f in qs["tensor"]:
                    f(e)
        self.q = {e: [] for e in ENGS}


class Ring:
    def __init__(self, bufs):
        self.bufs = bufs
        self.free = [[] for _ in bufs]
        self.i = 0

    def get(self):
        i = self.i
        self.i = (i + 1) % len(self.bufs)
        fr = self.free[i]
        self.free[i] = []
        return i, self.bufs[i], fr

    def release(self, i, tok):
        if isinstance(tok, (list, tuple)):
            self.free[i].extend(tok)
        else:
            self.free[i].append(tok)


def build_nc(debug=False):
    nc = bass.Bass("TRN2", target_bir_lowering=False)
    x = nc.dram_tensor("x", [2, S, D], F32, kind="ExternalInput").ap()
    w_in = nc.dram_tensor("w_in", [D, 2304], F32, kind="ExternalInput").ap()
    w_out = nc.dram_tensor("w_out", [D, D], F32, kind="ExternalInput").ap()
    w_up = nc.dram_tensor("w_up", [D, DFF], F32, kind="ExternalInput").ap()
    w_dn = nc.dram_tensor("w_dn", [DFF, D], F32, kind="ExternalInput").ap()
    small_d = nc.dram_tensor("small", [128, 32], F32, kind="ExternalInput").ap()
    gfin_d = nc.dram_tensor("gfin", [1, D], F32, kind="ExternalInput").ap()
    cmat_d = nc.dram_tensor("cmat", [128, 512], F32, kind="ExternalInput").ap()
    ropeC_d = nc.dram_tensor("ropeC", [128, S], F32, kind="ExternalInput").ap()
    ropeS_d = nc.dram_tensor("ropeS", [128, S], F32, kind="ExternalInput").ap()
    rb33_d = nc.dram_tensor("rb33", [33, 8], F32, kind="ExternalInput").ap()
    oh33_d = nc.dram_tensor("oh33", [33, UW], F32, kind="ExternalInput").ap()
    out = nc.dram_tensor("out", [2, S, D], F32, kind="ExternalOutput").ap()
    vscr = nc.dram_tensor("vscr", [8, UW], BF16)
    hscr = nc.dram_tensor("hscr", [2, S, D], F32, kind="ExternalOutput" if debug else "Internal").ap()
    if debug:
        dbg = {
            "nT": nc.dram_tensor("dbg_nT", [128, 8 * S], BF16, kind="ExternalOutput").ap(),
            "QAT": nc.dram_tensor("dbg_QAT", [128, 4 * S], BF16, kind="ExternalOutput").ap(),
            "KAT": nc.dram_tensor("dbg_KAT", [128, S], BF16, kind="ExternalOutput").ap(),
            "QBT": nc.dram_tensor("dbg_QBT", [128, 4 * S], BF16, kind="ExternalOutput").ap(),
            "KBT": nc.dram_tensor("dbg_KBT", [128, 4 * S], BF16, kind="ExternalOutput").ap(),
            "VA": nc.dram_tensor("dbg_VA", [128, NT * 2 * 65], BF16, kind="ExternalOutput").ap(),
            "VB": nc.dram_tensor("dbg_VB", [128, NT * 8 * 65], BF16, kind="ExternalOutput").ap(),
            "Y": nc.dram_tensor("dbg_Y", [128, NT * 1024], BF16, kind="ExternalOutput").ap(),
            "ssqh": nc.dram_tensor("dbg_ssqh", [128, 256], F32, kind="ExternalOutput").ap(),
            "Hk": nc.dram_tensor("dbg_Hk", [128, HW], BF16, kind="ExternalOutput").ap(),
            "sm": nc.dram_tensor("dbg_sm", [128, 512], F32, kind="ExternalOutput").ap(),
        }
        d_dbg = None

    ARENA = 212800
    arena = nc.alloc_sbuf_tensor("arena", [128, ARENA], U8)

    def view(off, n, dt, pat=None, **kw):
        esz = 4 if dt == F32 else 2
        assert off % 32 == 0 and off + n * esz <= ARENA, (off, n)
        v = arena[:, off:off + n * esz].bitcast(dt)
        if pat:
            v = v.rearrange(pat, **kw)
        return v

    class Alloc:
        def __init__(self, base):
            self.off = base

        def take(self, nbytes):
            o = self.off
            self.off += (nbytes + 31) // 32 * 32
            return o

    al = Alloc(0)
    cb = view(al.take(1024), 512, BF16, "p (a b) -> p a b", a=4)
    ident, Jm, onesblk, Rmat = cb[:, 0, :], cb[:, 1, :], cb[:, 2, :], cb[:, 3, :]
    sm = view(al.take(2048), 512, F32)
    GIN, GOUT, GMLP, GQ, GK, EPSC, GQ8 = 0, 8, 16, 24, 25, 26, 27
    MLP_BASE = al.off
    Win = view(al.take(8 * 2304 * 2), 8 * 2304, BF16, "p (c n) -> p c n", c=8)
    QAT = view(al.take(4 * S * 2), 4 * S, BF16, "p (c n) -> p c n", c=4)
    KAT = view(al.take(S * 2), S, BF16)
    QBT = view(al.take(4 * S * 2), 4 * S, BF16, "p (c n) -> p c n", c=4)
    KBT = view(al.take(4 * S * 2), 4 * S, BF16, "p (c n) -> p c n", c=4)
    VA = view(al.take(NT * 2 * 65 * 2), NT * 2 * 65, BF16, "p (t h e) -> p t h e", t=NT, h=2)
    VB = view(al.take(NT * 8 * 65 * 2), NT * 8 * 65, BF16, "p (t h e) -> p t h e", t=NT, h=8)
    WOUT_OFF = al.off
    Wout = view(al.take(8 * 1024 * 2), 8 * 1024, BF16, "p (c n) -> p c n", c=8)
    R0 = al.off
    ar = Alloc(R0)
    nT = view(ar.take(8 * S * 2), 8 * S, BF16, "p (c n) -> p c n", c=8)
    ropeCb = [view(ar.take(2048), 512, F32) for _ in range(2)]
    ropeSb = [view(ar.take(2048), 512, F32) for _ in range(2)]
    NXB = 3
    NXS = 4
    xs = [view(ar.take(4096), 1024, F32) for _ in range(NXS)]
    xnb = [view(ar.take(2048), 1024, BF16) for _ in range(NXB)]
    sqb = [view(ar.take(1024), 512, BF16) for _ in range(2)]
    rtb = [view(ar.take(2048), 512, F32) for _ in range(2)]
    rinv = rtb
    qnb = [view(ar.take(1024), 512, BF16) for _ in range(2)]
    t1b = [view(ar.take(2048), 512, F32) for _ in range(2)]
    t2b = [view(ar.take(2048), 512, F32) for _ in range(2)]
    assert ar.off <= ARENA, ar.off
    aa = Alloc(R0)
    Hk = [view(aa.take(HW * 2), HW, BF16) for _ in range(4)]
    Y = view(aa.take(NT * 1024 * 2), NT * 1024, BF16, "p (t n) -> p t n", t=NT)
    NP = 4
    Pb = [view(aa.take(2048), 1024, BF16) for _ in range(NP)]
    yfb = [view(aa.take(1024), 256, F32, "p (a b) -> p a b", a=4) for _ in range(2)]
    ysq = view(aa.take(1024), 256, F32, "p (a b) -> p a b", a=4)
    rdenb = [view(aa.take(32), 4, F32) for _ in range(2)]
    ocp = [view(aa.take(1056), 260, F32, "p (a b) -> p a b", b=65) for _ in range(2)]
    ssqh = view(aa.take(2 * 8 * 16 * 4), 2 * 8 * 16, F32, "p (k h t) -> p k h t", k=2, h=8)
    assert aa.off <= ARENA, aa.off
    Y_off_end = R0 + 4 * ((HW * 2 + 31) // 32 * 32) + NT * 1024 * 2
    ap_ = Alloc(R0)
    mixT = [view(ap_.take(2048), 1024, BF16, "p (c n) -> p c n", c=8) for _ in range(2)]
    xr = [view(ap_.take(4096), 1024, F32) for _ in range(2)]
    hst = [view(ap_.take(4096), 1024, F32) for _ in range(2)]
    junkb = view(ap_.take(1024), 512, BF16)
    assert ap_.off <= R0 + 4 * (HW * 2)
    QKV0 = 2048 + 1024 + 8 * 2304 * 2
    as_ = Alloc(QKV0)
    cmat_f = view(as_.take(2048), 512, F32)
    rb33 = view(as_.take(32), 8, F32)
    oh33 = view(as_.take(UW * 4), UW, F32)
    vsb = view(as_.take(UW * 2), UW, BF16)
    as2 = Alloc(QKV0)
    NHB = 4
    hkst = [view(as2.take(HW * 2), HW, BF16) for _ in range(8)]
    tkst = [view(as2.take(HW * 2), HW, BF16) for _ in range(NHB)]
    assert as2.off <= WOUT_OFF
    assert as_.off <= R0
    am = Alloc(MLP_BASE)
    Wup = view(am.take(8 * DFF * 2), 8 * DFF, BF16, "p (c n) -> p c n", c=8)
    Wdn = view(am.take(32 * D * 2), 32 * D, BF16, "p (c n) -> p c n", c=32)
    gfin = view(am.take(4096), 1024, F32)
    G = 256
    aT = view(am.take(32 * G * 2), 32 * G, BF16, "p (c n) -> p c n", c=32)
    hnT = [view(am.take(8 * G * 2), 8 * G, BF16, "p (c n) -> p c n", c=8) for _ in range(2)]
    NHT = 6
    hT = [view(am.take(4096), 1024, F32) for _ in range(NHT)]
    hnb = [view(am.take(2048), 1024, BF16) for _ in range(2)]
    rst = [view(am.take(1024), G, F32) for _ in range(2)]
    ost = [view(am.take(4096), 1024, F32) for _ in range(2)]
    junkm = view(am.take(2048), 1024, BF16)
    assert am.off <= ARENA, am.off

    psall = nc.alloc_psum_tensor("psall", [128, 4096], F32)

    def psb(i):
        return psall[:, i * 512:(i + 1) * 512]

    def psbf(i):
        return psall[:, i * 512:(i + 1) * 512].bitcast(BF16)

    sc = Sched(nc)
    ndma = [0]

    def dsem():
        ndma[0] += 1
        return DmaSem(nc, "d%d" % ndma[0])

    t_small = sc.dma("sync", sm[:, 0:32], small_d, dsem())
    t_cm = sc.dma("sync", cmat_f, cmat_d, dsem())
    t_rb = sc.dma("sync", rb33[0:33, :], rb33_d, dsem())
    t_oh = sc.dma("sync", oh33[0:33, :], oh33_d, dsem())
    t_cb = sc.op("vector", lambda e: e.tensor_copy(out=cb.rearrange("p a b -> p (a b)"), in_=cmat_f), [t_cm])
    t_eps = sc.op("vector", lambda e: e.memset(sm[:, EPSC:EPSC + 1], EPS), [t_small])
    t_gq8 = sc.op("vector", lambda e: e.tensor_scalar_mul(out=sm[:, GQ8:GQ8 + 1], in0=sm[:, GQ:GQ + 1], scalar1=0.125), [t_small])
    t_v = []
    for j in range(UW // 512):
        tm = sc.op("tensor", lambda e, j=j: e.matmul(psb(j)[0:8, :], lhsT=rb33[0:33, :], rhs=oh33[0:33, j * 512:(j + 1) * 512],
                                                     start=True, stop=True), [t_rb, t_oh])
        t_v.append(sc.op("vector", lambda e, j=j: e.tensor_copy(out=vsb[0:8, j * 512:(j + 1) * 512], in_=psb(j)[0:8, :]), [tm]))
    d_v = dsem()
    t_vscr = sc.dma("sync", vscr.ap(), vsb[0:8, :], d_v, t_v)
    ps_free = [[t_v[j]] if j < UW // 512 else [] for j in range(8)]
    d_win = dsem()
    t_win = None
    for c in range(8):
        t_win = sc.dma("gpsimd", Win[:, c, :], w_in[c * 128:(c + 1) * 128, :], d_win)
    t_win = [t_win]
    d_wout = dsem()
    t_wout = None
    for c in range(8):
        t_wout = sc.dma("gpsimd", Wout[:, c, :], w_out[c * 128:(c + 1) * 128, :], d_wout)
    t_wout = [t_wout]
    setup_done = t_win + t_wout + [t_vscr, t_cb, t_eps, t_gq8] + t_v

    d_x = [dsem() for _ in range(4)]
    d_ropeC = [dsem() for _ in range(2)]
    d_ropeS = [dsem() for _ in range(2)]
    d_hk = [dsem() for _ in range(4)]
    d_xr = [dsem() for _ in range(2)]
    d_hs = [dsem() for _ in range(2)]

    region_free = list(setup_done)
    SX, RX, TX = 128, 160, 192
    SA, RA, TA = 224, 256, 288

    for s in range(2):
        psr = Ring([3, 4, 5, 6, 7])
        psr.free = [list(ps_free[b]) for b in psr.bufs]
        psr_p = Ring([0, 1])
        psr_p.free = [list(ps_free[b]) for b in psr_p.bufs]
        psr_s = Ring([2])
        psr_s.free = [list(ps_free[b]) for b in psr_s.bufs]
        xfree = [list(region_free) for _ in range(NXS)]
        xnfree = [list(region_free) for _ in range(NXB)]
        t_nT = [None] * NT
        qk_tokens, a_tokens = [], []
        t_vones = sc.op("gpsimd", lambda e: e.memset(VA[:, :, :, 64:65], 1.0), [region_free])
        t_vones2 = sc.op("gpsimd", lambda e: e.memset(VB[:, :, :, 64:65], 1.0), [region_free])
        v_tokens = [t_vones, t_vones2]
        ropefree = [list(region_free), list(region_free)]
        rope_tok = {}
        rope_users = {0: [], 1: [], 2: [], 3: []}
        ab_free = {"sq": [[], []], "rt": [[], []], "ri": [[], []], "qn": [[], []], "t1": [[], []], "t2": [[], []]}
        acount = [0]

        def tg_dep(tg):
            r = t_nT[tg * 4:(tg + 1) * 4]
            assert all(t is not None for t in r), tg
            return r

        def tile_unit(i):
            b = i % NXB
            bx = i % NXS
            tl = sc.dma("sync", xs[bx], x[s, i * 128:(i + 1) * 128, :], d_x[bx], xfree[bx])
            ta = sc.op("scalar", lambda e: e.activation(out=xnb[b], in_=xs[bx], func=AF.Square,
                                                        accum_out=sm[:, SX + i:SX + i + 1]), [tl, xnfree[b]])
            tb_ = sc.op("scalar", lambda e: e.activation(out=sm[:, TX + i:TX + i + 1], in_=sm[:, SX + i:SX + i + 1], func=AF.Ln,
                                                         bias=sm[:, EPSC:EPSC + 1], scale=1.0 / D), [ta, t_eps])
            tr = sc.op("scalar", lambda e: e.activation(out=sm[:, RX + i:RX + i + 1], in_=sm[:, TX + i:TX + i + 1], func=AF.Exp,
                                                        scale=-0.5), [tb_])
            tn = sc.op("vector", lambda e: e.tensor_scalar_mul(out=xnb[b], in0=xs[bx], scalar1=sm[:, RX + i:RX + i + 1]), [tr, ta])
            xfree[bx] = [tn]
            yield
            bi, bank, fr = psr.get()
            tt = None
            for c in range(8):
                tt = sc.op("tensor", lambda e, c=c: e.transpose(psbf(bank)[:, c * 128:(c + 1) * 128],
                                                                xnb[b][:, c * 128:(c + 1) * 128], ident),
                           [tn, fr, t_cb, region_free], sig=(c == 7))
            xnfree[b] = [tt]
            te = sc.op("vector", lambda e: e.tensor_tensor(
                out=nT[:, :, i * 128:(i + 1) * 128], in0=psbf(bank).rearrange("p (c n) -> p c n", c=8),
                in1=sm[:, GIN:GIN + 8].unsqueeze(2).to_broadcast([128, 8, 128]), op=ALU.mult), [tt, t_small])
            psr.release(bi, te)
            t_nT[i] = te
            yield

        def qkb_unit(j, tg):
            isq = j < 4
            col0 = 640 + j * 128 if isq else 1152 + (j - 4) * 128
            bi, bank, fr = psr.get()
            tm = None
            for c in range(8):
                tm = sc.op("tensor", lambda e, c=c: e.matmul(
                    psb(bank), lhsT=Win[:, c, col0:col0 + 128], rhs=nT[:, c, tg * 512:(tg + 1) * 512],
                    start=(c == 0), stop=(c == 7)), [tg_dep(tg), fr, t_win, region_free], sig=(c == 7))
            dst = (QBT if isq else KBT)[:, j % 4, tg * 512:(tg + 1) * 512]
            if (j + tg) % 2 == 0:
                te = sc.op("vector", lambda e: e.tensor_scalar_mul(out=dst, in0=psb(bank), scalar1=0.125 if isq else 1.0), [tm])
            else:
                te = sc.op("scalar", lambda e: e.activation(out=dst, in_=psb(bank), func=AF.Copy, scale=0.125 if isq else 1.0), [tm])
            psr.release(bi, te)
            qk_tokens.append(te)
            yield

        def v_unit(i, which):
            assert t_nT[i] is not None
            bi, bank, fr = psr.get()
            tm = None
            if which == "B":
                for c in range(8):
                    tm = sc.op("tensor", lambda e, c=c: e.matmul(
                        psb(bank), lhsT=nT[:, c, i * 128:(i + 1) * 128], rhs=Win[:, c, 1792:2304],
                        start=(c == 0), stop=(c == 7)), [t_nT[i], fr, t_win, region_free], sig=(c == 7))
                src = psb(bank).rearrange("p (h d) -> p h d", h=8)
                dst = VB[:, i, :, 0:64]
                tv = t_vones2
            else:
                for c in range(8):
                    tm = sc.op("tensor", lambda e, c=c: e.matmul(
                        psb(bank)[:, 0:128], lhsT=nT[:, c, i * 128:(i + 1) * 128], rhs=Win[:, c, 1664:1792],
                        start=(c == 0), stop=(c == 7)), [t_nT[i], fr, t_win, region_free], sig=(c == 7))
                src = psb(bank)[:, 0:128].rearrange("p (h d) -> p h d", h=2)
                dst = VA[:, i, :, 0:64]
                tv = t_vones
            if i % 2 == 0:
                te = sc.op("vector", lambda e: e.tensor_copy(out=dst, in_=src), [tm, tv])
            else:
                te = sc.op("scalar", lambda e: e.activation(out=dst, in_=src, func=AF.Copy), [tm, tv])
            psr.release(bi, te)
            v_tokens.append(te)
            yield

        def a_unit(j, tg):
            if tg not in rope_tok:
                rbk_ = tg % 2
                rope_tok[tg] = (sc.dma("sync", ropeCb[rbk_], ropeC_d[:, tg * 512:(tg + 1) * 512], d_ropeC[rbk_], ropefree[rbk_]),
                                sc.dma("sync", ropeSb[rbk_], ropeS_d[:, tg * 512:(tg + 1) * 512], d_ropeS[rbk_], ropefree[rbk_]))
            rbk = tg % 2
            tlc, tls = rope_tok[tg]
            col0 = j * 128 if j < 4 else 512
            gcol = GQ8 if j < 4 else GK
            dst = QAT[:, j, tg * 512:(tg + 1) * 512] if j < 4 else KAT[:, tg * 512:(tg + 1) * 512]
            k = acount[0] % 2
            acount[0] += 1
            bi, bank, fr = psr_p.get()
            tm = None
            for c in range(8):
                tm = sc.op("tensor", lambda e, c=c: e.matmul(
                    psb(bank), lhsT=Win[:, c, col0:col0 + 128], rhs=nT[:, c, tg * 512:(tg + 1) * 512],
                    start=(c == 0), stop=(c == 7)), [tg_dep(tg), fr, t_win, region_free], sig=(c == 7))
            tsq = sc.op("scalar", lambda e: e.activation(out=sqb[k], in_=psb(bank), func=AF.Square), [tm, ab_free["sq"][k]])
            yield
            b2i, bank2, fr2 = psr_s.get()
            tss = sc.op("tensor", lambda e: e.matmul(psb(bank2), lhsT=onesblk, rhs=sqb[k], start=True, stop=True), [tsq, fr2, t_cb])
            ab_free["sq"][k] = [tss]
            trt = sc.op("scalar", lambda e: e.activation(out=rtb[k], in_=psb(bank2), func=AF.Ln,
                                                         bias=sm[:, EPSC:EPSC + 1], scale=1.0 / 64), [tss, ab_free["rt"][k], ab_free["ri"][k], t_eps])
            psr_s.release(b2i, trt)
            tri = sc.op("scalar", lambda e: e.activation(out=rinv[k], in_=rtb[k], func=AF.Exp, scale=-0.5), [trt, ab_free["ri"][k]])
            ab_free["rt"][k] = [tri]
            tqn = sc.op("vector", lambda e: e.scalar_tensor_tensor(
                out=qnb[k], in0=psb(bank), scalar=sm[:, gcol:gcol + 1], in1=rinv[k], op0=ALU.mult, op1=ALU.mult),
                [tri, tm, ab_free["qn"][k], t_gq8])
            ab_free["ri"][k] = [tqn]
            psr_p.release(bi, tqn)
            yield
            b3i, bank3, fr3 = psr_s.get()
            trot = sc.op("tensor", lambda e: e.matmul(psb(bank3), lhsT=Rmat, rhs=qnb[k], start=True, stop=True), [tqn, fr3, t_cb])
            tt1 = sc.op("gpsimd", lambda e: e.tensor_tensor(out=t1b[k], in0=qnb[k], in1=ropeCb[rbk], op=ALU.mult),
                        [tqn, tlc, ab_free["t1"][k]])
            tt2 = sc.op("vector", lambda e: e.tensor_tensor(out=t2b[k], in0=psb(bank3), in1=ropeSb[rbk], op=ALU.mult),
                        [trot, tls, ab_free["t2"][k]])
            psr_s.release(b3i, tt2)
            ab_free["qn"][k] = [trot, tt1]
            tfin = sc.op("gpsimd", lambda e: e.tensor_tensor(out=dst, in0=t1b[k], in1=t2b[k], op=ALU.add), [tt1, tt2])
            ab_free["t1"][k] = [tfin]
            ab_free["t2"][k] = [tfin]
            a_tokens.append(tfin)
            rope_users[tg].extend([tt1, tt2])
            if len(rope_users[tg]) == 10:
                ropefree[rbk] = list(rope_users[tg])
            yield

        def rdy_tg(tg):
            return lambda: all(t is not None for t in t_nT[4 * tg:4 * tg + 4])

        def rdy_tile(i):
            return lambda: t_nT[i] is not None

        def rdy_xbuf(i):
            return lambda: i < NXB or t_nT[i - NXB] is not None

        pending = [(tile_unit(i), False, rdy_xbuf(i)) for i in range(4)]
        for tg in range(4):
            nxt_tiles = [(tile_unit(i), False, rdy_xbuf(i)) for i in range(4 * (tg + 1), 4 * (tg + 2))] if tg < 3 else []
            items = []
            for j in range(4):
                items += [(a_unit(j, tg), True, rdy_tg(tg)), (qkb_unit(j, tg), False, rdy_tg(tg)), (qkb_unit(4 + j, tg), False, rdy_tg(tg))]
                if j < len(nxt_tiles):
                    items.append(nxt_tiles[j])
                items += [(v_unit(4 * tg + j, "B"), False, rdy_tile(4 * tg + j))]
            items += [(a_unit(4, tg), True, rdy_tg(tg))] + [(v_unit(4 * tg + j, "A"), False, rdy_tile(4 * tg + j)) for j in range(4)]
            pending += items
        active = []
        WIN = 9
        while pending or active:
            while pending and len(active) < WIN:
                nA = sum(1 for a_ in active if a_[1])
                pick = None
                for idx_, it_ in enumerate(pending):
                    if it_[2]() and not (it_[1] and nA >= 2):
                        pick = idx_
                        break
                    if idx_ > 12:
                        break
                if pick is None:
                    break
                active.append(pending.pop(pick))
            assert active, "unit scheduler stuck"
            for a_ in list(active):
                try:
                    next(a_[0])
                except StopIteration:
                    active.remove(a_)
        proj_done = qk_tokens + v_tokens + a_tokens
        if debug and s == 0:
            d_dbg = dsem()
            fl = lambda v, p: v.rearrange(p)
            dt_ = [sc.dma("sync", dbg["nT"], fl(nT, "p c n -> p (c n)"), dsem(), proj_done),
                   sc.dma("sync", dbg["QAT"], fl(QAT, "p c n -> p (c n)"), dsem(), proj_done),
                   sc.dma("sync", dbg["KAT"], KAT, dsem(), proj_done),
                   sc.dma("sync", dbg["QBT"], fl(QBT, "p c n -> p (c n)"), dsem(), proj_done),
                   sc.dma("sync", dbg["KBT"], fl(KBT, "p c n -> p (c n)"), dsem(), proj_done),
                   sc.dma("sync", dbg["VA"], fl(VA, "p t h e -> p (t h e)"), dsem(), proj_done),
                   sc.dma("sync", dbg["VB"], fl(VB, "p t h e -> p (t h e)"), dsem(), proj_done)]
            proj_done = proj_done + dt_
        ps_free = [[] for _ in range(8)]
        for rg in (psr, psr_p, psr_s):
            for bidx, fr_ in zip(rg.bufs, rg.free):
                ps_free[bidx] = list(fr_)
        if s == 1:
            d_up_early = [dsem() for _ in range(4)]
            early_up = {}
            for q4 in range(4):
                tk = None
                for c in range(4):
                    tk = sc.dma("gpsimd", Wup[:, c, q4 * 1024:(q4 + 1) * 1024],
                                w_up[c * 128:(c + 1) * 128, q4 * 1024:(q4 + 1) * 1024], d_up_early[q4], proj_done)
                early_up[q4] = tk

        NSB = 3
        S_free = [list(ps_free[2 * b]) + list(ps_free[2 * b + 1]) for b in range(NSB)]
        O_free = [list(ps_free[6]), list(ps_free[7])]
        P_free = [list(proj_done) for _ in range(NP)]
        hk_free = [list(proj_done) for _ in range(4)]
        yf_free = [list(proj_done), list(proj_done)]
        ysq_free = [list(proj_done)]
        rden_free = [list(proj_done), list(proj_done)]
        ocp_free = [list(proj_done), list(proj_done)]
        y_tokens = []
        ssq_tokens = []
        hk_dma = {}
        hk_tok = {}
        obank = [6, 7]

        def load_hk(pair):
            for half in range(2):
                slot = (2 * pair + half) % 4
                h = 2 * pair + half
                hk_tok[(pair, half)] = sc.dma("sync", Hk[slot], bass.AP(vscr, h * UW, [[1, 128], [1, HW]]), d_hk[slot],
                                              [hk_free[slot], t_vscr])

        steps = []
        for kind in ("A", "B"):
            for pair in range(4):
                for qc in range(4):
                    kts = list(range(16)) if kind == "A" else list(range(max(0, 4 * qc - 8), min(15, 4 * qc + 11) + 1))
                    for n, kt in enumerate(kts):
                        steps.append(dict(kind=kind, pair=pair, qc=qc, kt=kt, first=(n == 0), last=(n == len(kts) - 1)))

        def KT(st, half):
            kt = st["kt"]
            if st["kind"] == "A":
                return KAT[half * 64:(half + 1) * 64, kt * 128:(kt + 1) * 128]
            return KBT[half * 64:(half + 1) * 64, st["pair"], kt * 128:(kt + 1) * 128]

        def qrange(st):
            if st["kind"] == "A":
                return 0, 4
            qbs = [qb for qb in range(4) if abs(st["kt"] - 4 * st["qc"] - qb) <= 8]
            assert qbs and qbs == list(range(qbs[0], qbs[-1] + 1))
            return qbs[0], qbs[-1] + 1

        def QT(st, half):
            qc = st["qc"]
            lo, hi = qrange(st)
            src = QAT if st["kind"] == "A" else QBT
            return src[half * 64:(half + 1) * 64, st["pair"], qc * 512 + lo * 128:qc * 512 + hi * 128]

        def Vt(st, half):
            if st["kind"] == "A":
                return VA[:, st["kt"], half, :]
            return VB[:, st["kt"], 2 * st["pair"] + half, :]

        qk_tok = {}

        def emit_qk(i):
            st = steps[i]
            buf = i % NSB
            deps = [S_free[buf], proj_done]
            S_free[buf] = []
            t = None
            isB = st["kind"] == "B"
            lo, hi = qrange(st)
            for half in range(2):
                t = sc.op("tensor", lambda e, half=half, buf=buf, st=st, lo=lo, hi=hi: e.matmul(
                    psb(2 * buf + half)[:, lo * 128:hi * 128], lhsT=KT(st, half), rhs=QT(st, half), start=True, stop=(not isB)), deps,
                    sig=(half == 1 and not isB))
            if isB:
                kt, qc, pair = st["kt"], st["qc"], st["pair"]
                c0 = 1408 - (kt * 128 - qc * 512)
                assert 0 <= c0 and c0 + 512 <= HW
                for half in range(2):
                    slot = (2 * pair + half) % 4
                    t = sc.op("tensor", lambda e, half=half, buf=buf, slot=slot, c0=c0, lo=lo, hi=hi: e.matmul(
                        psb(2 * buf + half)[:, lo * 128:hi * 128], lhsT=Jm, rhs=Hk[slot][:, c0 + lo * 128:c0 + hi * 128],
                        start=False, stop=True), [hk_tok[(pair, half)]], sig=(half == 1))
                st["mul"] = t
            qk_tok[i] = t

        def epilogue(st):
            kind, pair, qc = st["kind"], st["pair"], st["qc"]
            heads = [pair, pair + 4] if kind == "A" else [2 * pair, 2 * pair + 1]
            tcps = []
            for half in range(2):
                O3 = psb(obank[half])[:, 0:260].rearrange("p (a b) -> p a b", b=65)
                tcp = sc.op("vector", lambda e, half=half, O3=O3: e.tensor_copy(out=ocp[half], in_=O3), [st["pv"][half], ocp_free[half]])
                O_free[half] = [tcp]
                tcps.append(tcp)
            trds = []
            for half in range(2):
                trds.append(sc.op("vector", lambda e, half=half: e.reciprocal(out=rdenb[half], in_=ocp[half][:, :, 64]),
                                  [tcps[half], rden_free[half]]))
            for half in range(2):
                h = heads[half]
                colb = h * 64 + (0 if kind == "A" else 512)
                ty = sc.op("gpsimd", lambda e, half=half, colb=colb, qc=qc: e.tensor_tensor(
                    out=Y[:, qc * 4:(qc + 1) * 4, colb:colb + 64], in0=ocp[half][:, :, 0:64],
                    in1=rdenb[half].unsqueeze(2).to_broadcast([128, 4, 64]), op=ALU.mult), [trds[half], proj_done])
                ocp_free[half] = [ty]
                rden_free[half] = [ty]
                y_tokens.append(ty)

        load_hk(0)
        load_hk(1)
        LOOK = 2
        for i in range(min(LOOK, len(steps))):
            emit_qk(i)
        for i, st in enumerate(steps):
            if i + LOOK < len(steps):
                emit_qk(i + LOOK)
            buf = i % NSB
            pi = i % NP
            elo, ehi = qrange(st)
            if (elo, ehi) == (0, 4):
                tex = sc.op("scalar", lambda e, buf=buf, pi=pi: e.activation(
                    out=Pb[pi], in_=psall[:, buf * 1024:(buf + 1) * 1024], func=AF.Exp), [qk_tok[i], P_free[pi]])
            else:
                tex = sc.op("scalar", lambda e, buf=buf, pi=pi, elo=elo, ehi=ehi: e.activation(
                    out=Pb[pi].rearrange("p (h n) -> p h n", h=2)[:, :, elo * 128:ehi * 128],
                    in_=psall[:, buf * 1024:(buf + 1) * 1024].rearrange("p (h n) -> p h n", h=2)[:, :, elo * 128:ehi * 128],
                    func=AF.Exp), [qk_tok[i], P_free[pi]])
            S_free[buf] = [tex]
            tready = tex
            tpv = None
            st["pv"] = [None, None]
            plo, phi = qrange(st)
            if st["first"]:
                assert plo == 0
            for half in range(2):
                for qb in range(plo, phi):
                    deps = [tready]
                    if st["first"]:
                        deps.append(O_free[half])
                    tpv = sc.op("tensor", lambda e, half=half, pi=pi, qb=qb, st=st: e.matmul(
                        psb(obank[half])[:, qb * 65:(qb + 1) * 65], lhsT=Pb[pi][:, half * 512 + qb * 128:half * 512 + (qb + 1) * 128],
                        rhs=Vt(st, half), start=(st["first"] and qb == 0), stop=st["last"], skip_group_check=True),
                        deps, sig=(qb == phi - 1))
                if st["first"]:
                    O_free[half] = []
                st["pv"][half] = tpv
            P_free[pi] = [tpv]
            if st["last"]:
                epilogue(st)
                if st["kind"] == "B" and st["qc"] == 3:
                    pair = st["pair"]
                    for half in range(2):
                        slot = (2 * pair + half) % 4
                        hk_free[slot] = [st["mul"]]
                    if pair + 2 < 4:
                        load_hk(pair + 2)
        attn_done = y_tokens + [Tok(sc.sem["tensor"], sc.cnt["tensor"], "tensor", sc.idx["tensor"])]
        if debug and s == 0:
            dt_ = [sc.dma("sync", dbg["Y"], Y.rearrange("p t n -> p (t n)"), dsem(), attn_done),
                   sc.dma("sync", dbg["Hk"], Hk[3], dsem(), attn_done)]
            attn_done = attn_done + dt_

        if s == 1:
            FLATE = (WOUT_OFF - (MLP_BASE + 8 * DFF * 2)) // 2048
            d_dn_e = [dsem() for _ in range(4)]
            early_dn = {}
            for f8 in range(4):
                tk = None
                for f in range(f8 * 8, min(f8 * 8 + 8, FLATE)):
                    tk = sc.dma("gpsimd", Wdn[:, f, :], w_dn[f * 128:(f + 1) * 128, :], d_dn_e[f8], attn_done)
                early_dn[f8] = tk
            d_up7 = [dsem() for _ in range(4)]
            early_up7 = {}
            for q4 in range(4):
                tk = None
                for c in range(4, 8):
                    tk = sc.dma("gpsimd", Wup[:, c, q4 * 1024:(q4 + 1) * 1024],
                                w_up[c * 128:(c + 1) * 128, q4 * 1024:(q4 + 1) * 1024], d_up7[q4], attn_done)
                early_up7[q4] = tk

        psr_t = Ring([0, 1])
        psr_t.free = [list(attn_done) for _ in range(2)]
        psr = Ring([2, 3, 4, 5, 6, 7])
        psr.free = [list(attn_done) for _ in range(6)]
        mix_free = [list(attn_done), list(attn_done)]
        xr_free = [list(attn_done), list(attn_done)]
        hst_free = [list(attn_done), list(attn_done)]
        store_toks = []
        tstate = {}

        def p4_T(i):
            b = i % 2
            tlx = sc.dma("sync", xr[b], x[s, i * 128:(i + 1) * 128, :], d_xr[b], xr_free[b])
            tq_ = []
            for ab in range(2):
                tq_.append(sc.op("scalar", lambda e, ab=ab: e.activation(
                    out=junkb, in_=Y[:, i, ab * 512:(ab + 1) * 512], func=AF.Square,
                    accum_out=sm[:, SA + 2 * i + ab:SA + 2 * i + ab + 1]), [y_tokens, attn_done]))
            tln = sc.op("scalar", lambda e: e.activation(out=sm[:, TA + 2 * i:TA + 2 * i + 2], in_=sm[:, SA + 2 * i:SA + 2 * i + 2], func=AF.Ln,
                                                         bias=sm[:, EPSC:EPSC + 1], scale=1.0 / 512), [tq_])
            trs = sc.op("scalar", lambda e: e.activation(out=sm[:, RA + 2 * i:RA + 2 * i + 2], in_=sm[:, TA + 2 * i:TA + 2 * i + 2], func=AF.Exp,
                                                         scale=-0.5), [tln])
            bi, bank, fr = psr_t.get()
            tt = None
            for c in range(8):
                tt = sc.op("tensor", lambda e, c=c: e.transpose(psbf(bank)[:, c * 128:(c + 1) * 128],
                                                                Y[:, i, c * 128:(c + 1) * 128], ident),
                           [y_tokens, fr], sig=(c == 7))
            te = sc.op("vector", lambda e: e.tensor_tensor(
                out=mixT[b], in0=psbf(bank).rearrange("p (c n) -> p c n", c=8),
                in1=sm[:, GOUT:GOUT + 8].unsqueeze(2).to_broadcast([128, 8, 128]), op=ALU.mult), [tt, mix_free[b]])
            psr_t.release(bi, te)
            tstate[i] = (tlx, trs, te)

        def p4_M(i):
            b = i % 2
            tlx, trs, te = tstate[i]
            last_mm = None
            prev = []
            for half in range(2):
                banks = []
                for ab in range(2):
                    bi2, bank2, fr2 = psr.get()
                    tm = None
                    for cc in range(4):
                        c = ab * 4 + cc
                        tm = sc.op("tensor", lambda e, c=c, half=half, bank2=bank2, cc=cc: e.matmul(
                            psb(bank2), lhsT=mixT[b][:, c, :], rhs=Wout[:, c, half * 512:(half + 1) * 512],
                            start=(cc == 0), stop=(cc == 3)), [te, fr2, t_wout], sig=(cc == 3))
                    banks.append((bi2, bank2, tm))
                    last_mm = tm
                (ba_i, ba, tma), (bb_i, bb, tmb) = banks
                hs = hst[b][:, half * 512:(half + 1) * 512]
                th1 = sc.op("vector", lambda e, ba=ba, hs=hs, half=half: e.scalar_tensor_tensor(
                    out=hs, in0=psb(ba), scalar=sm[:, RA + 2 * i:RA + 2 * i + 1], in1=xr[b][:, half * 512:(half + 1) * 512],
                    op0=ALU.mult, op1=ALU.add), [tma, tlx, trs, hst_free[b]])
                psr.release(ba_i, th1)
                th2 = sc.op("vector", lambda e, bb=bb, hs=hs: e.scalar_tensor_tensor(
                    out=hs, in0=psb(bb), scalar=sm[:, RA + 2 * i + 1:RA + 2 * i + 2], in1=hs, op0=ALU.mult, op1=ALU.add),
                    [tmb, th1])
                psr.release(bb_i, th2)
                prev.append(th2)
            mix_free[b] = [last_mm]
            xr_free[b] = list(prev)
            tst = sc.dma("sync", hscr[s, i * 128:(i + 1) * 128, :], hst[b], d_hs[b], prev)
            hst_free[b] = [tst]
            store_toks.append(tst)

        p4_T(0)
        for i in range(NT):
            if i + 1 < NT:
                p4_T(i + 1)
            p4_M(i)
        ps_free = [[] for _ in range(8)]
        for rg in (psr, psr_t):
            for bidx, fr_ in zip(rg.bufs, rg.free):
                ps_free[bidx] = list(fr_)
        region_free = store_toks + [Tok(sc.sem["tensor"], sc.cnt["tensor"], "tensor", sc.idx["tensor"])] + \
            [Tok(sc.sem["vector"], sc.cnt["vector"], "vector", sc.idx["vector"])]

    if debug:
        region_free = region_free + [sc.dma("sync", dbg["sm"], sm, dsem(), region_free)]
    all_done = region_free + [Tok(sc.sem[e], sc.cnt[e], e, sc.idx[e]) for e in ("scalar", "gpsimd")]
    d_g = dsem()
    d_h = [dsem() for _ in range(6)]
    d_o = [dsem() for _ in range(2)]
    t_gfin = sc.dma("sync", gfin, gfin_d.partition_broadcast(128), d_g, all_done)
    sc.wait_all("gpsimd", all_done)
    d_up = [dsem() for _ in range(4)]
    d_dn = [dsem() for _ in range(4)]
    t_wup = [[None] * 4 for _ in range(8)]
    t_wdn = [None] * 32
    for q4 in range(4):
        for c in range(8):
            t_wup[c][q4] = early_up[q4] if c < 4 else early_up7[q4]
    d_dn_late = dsem()
    tk = None
    for f in range(FLATE, 32):
        tk = sc.dma("gpsimd", Wdn[:, f, :], w_dn[f * 128:(f + 1) * 128, :], d_dn_late, all_done)
    for f in range(32):
        t_wdn[f] = early_dn[f // 8] if f < FLATE else tk

    SH, RH, TH = 320, 352, 384
    SO, RO, TO = 416, 448, 480
    psr = Ring(list(range(8)))
    psr.free = [list(all_done) for _ in range(8)]
    h_free = [list(all_done) for _ in range(6)]
    hn_free = [list(all_done), list(all_done)]
    hnT_free = [list(all_done), list(all_done)]
    rst_free = [list(all_done), list(all_done)]
    ost_free = [list(all_done), list(all_done)]
    hcount = [0]
    NG = 2 * S // G
    TPG = G // 128
    final_toks = []

    def load_h(gt):
        s_, i_ = divmod(gt, NT)
        k = hcount[0] % 6
        hcount[0] += 1
        t = sc.dma("sync", hT[k], hscr[s_, i_ * 128:(i_ + 1) * 128, :], d_h[k], h_free[k])
        return k, t

    def prepA(g):
        st_ = []
        for tl_ in range(TPG):
            gt = g * TPG + tl_
            k, tl = load_h(gt)
            b = gt % 2
            ta = sc.op("scalar", lambda e, b=b, k=k, gt=gt: e.activation(out=hnb[b], in_=hT[k], func=AF.Square,
                                                                         accum_out=sm[:, SH + gt:SH + gt + 1]), [tl, hn_free[b]])
            tb_ = sc.op("scalar", lambda e, gt=gt: e.activation(out=sm[:, TH + gt:TH + gt + 1], in_=sm[:, SH + gt:SH + gt + 1], func=AF.Ln,
                                                                bias=sm[:, EPSC:EPSC + 1], scale=1.0 / D), [ta])
            tr = sc.op("scalar", lambda e, gt=gt: e.activation(out=sm[:, RH + gt:RH + gt + 1], in_=sm[:, TH + gt:TH + gt + 1], func=AF.Exp,
                                                               scale=-0.5), [tb_])
            tn = sc.op("vector", lambda e, b=b, k=k, gt=gt: e.tensor_scalar_mul(out=hnb[b], in0=hT[k], scalar1=sm[:, RH + gt:RH + gt + 1]),
                       [tr, ta])
            h_free[k] = [tn]
            st_.append((b, tn))
        return st_

    def prepB(g, st_):
        gb = g % 2
        toks = []
        for tl_, (b, tn) in enumerate(st_):
            bi, bank, fr = psr.get()
            tt = None
            for c in range(8):
                tt = sc.op("tensor", lambda e, c=c, b=b, bank=bank: e.transpose(psbf(bank)[:, c * 128:(c + 1) * 128],
                                                                                hnb[b][:, c * 128:(c + 1) * 128], ident),
                           [tn, fr], sig=(c == 7))
            hn_free[b] = [tt]
            te = sc.op("vector", lambda e, gb=gb, tl_=tl_, bank=bank: e.tensor_tensor(
                out=hnT[gb][:, :, tl_ * 128:(tl_ + 1) * 128], in0=psbf(bank).rearrange("p (c n) -> p c n", c=8),
                in1=sm[:, GMLP:GMLP + 8].unsqueeze(2).to_broadcast([128, 8, 128]), op=ALU.mult), [tt, hnT_free[gb]])
            psr.release(bi, te)
            toks.append(te)
        return toks

    hn_ready = prepB(0, prepA(0))
    for g in range(NG):
        gb = g % 2
        nxtA = None
        up_toks = []
        last_up = None
        for f in range(32):
            if f == 16 and g + 1 < NG:
                nxtA = prepA(g + 1)
            bi, bank, fr = psr.get()
            tm = None
            for c in range(8):
                tm = sc.op("tensor", lambda e, c=c, f=f, gb=gb, bank=bank: e.matmul(
                    psb(bank)[:, 0:G], lhsT=Wup[:, c, f * 128:(f + 1) * 128], rhs=hnT[gb][:, c, :],
                    start=(c == 0), stop=(c == 7)), [hn_ready, fr, t_wup[c][f // 8]], sig=(c == 7))
            k = f % 2
            trl = sc.op("scalar", lambda e, k=k, bank=bank: e.activation(out=rst[k], in_=psb(bank)[:, 0:G], func=AF.Relu), [tm, rst_free[k]])
            psr.release(bi, trl)
            tsq2 = sc.op("gpsimd", lambda e, k=k, f=f: e.tensor_tensor(out=aT[:, f, :], in0=rst[k], in1=rst[k], op=ALU.mult), [trl])
            rst_free[k] = [tsq2]
            up_toks.append(tsq2)
            last_up = tm
        hnT_free[gb] = [last_up]
        nxt_ready = prepB(g + 1, nxtA) if g + 1 < NG else None
        for tl_ in range(TPG):
            gt = g * TPG + tl_
            s_, i_ = divmod(gt, NT)
            k, tlh = load_h(gt)
            ob = gt % 2
            halves = []
            for half in range(2):
                bi, bank, fr = psr.get()
                tm = None
                for f in range(32):
                    tm = sc.op("tensor", lambda e, f=f, tl_=tl_, half=half, bank=bank: e.matmul(
                        psb(bank), lhsT=aT[:, f, tl_ * 128:(tl_ + 1) * 128], rhs=Wdn[:, f, half * 512:(half + 1) * 512],
                        start=(f == 0), stop=(f == 31)), [up_toks[f], fr, t_wdn[f]], sig=(f == 31))
                to = sc.op("vector", lambda e, ob=ob, k=k, half=half, bank=bank: e.tensor_tensor(
                    out=ost[ob][:, half * 512:(half + 1) * 512], in0=psb(bank), in1=hT[k][:, half * 512:(half + 1) * 512], op=ALU.add),
                    [tm, tlh, ost_free[ob]])
                psr.release(bi, to)
                halves.append(to)
            h_free[k] = list(halves)
            ta = sc.op("scalar", lambda e, ob=ob, gt=gt: e.activation(out=junkm, in_=ost[ob], func=AF.Square,
                                                                      accum_out=sm[:, SO + gt:SO + gt + 1]), [halves, all_done])
            tb_ = sc.op("scalar", lambda e, gt=gt: e.activation(out=sm[:, TO + gt:TO + gt + 1], in_=sm[:, SO + gt:SO + gt + 1], func=AF.Ln,
                                                                bias=sm[:, EPSC:EPSC + 1], scale=1.0 / D), [ta])
            tr = sc.op("scalar", lambda e, gt=gt: e.activation(out=sm[:, RO + gt:RO + gt + 1], in_=sm[:, TO + gt:TO + gt + 1], func=AF.Exp,
                                                               scale=-0.5), [tb_])
            tf = sc.op("vector", lambda e, ob=ob, gt=gt: e.scalar_tensor_tensor(
                out=ost[ob], in0=ost[ob], scalar=sm[:, RO + gt:RO + gt + 1], in1=gfin, op0=ALU.mult, op1=ALU.mult), [tr, t_gfin, halves])
            tst = sc.dma("sync", out[s_, i_ * 128:(i_ + 1) * 128, :], ost[ob], d_o[ob], [tf])
            ost_free[ob] = [tst]
            final_toks.append(tst)
        hn_ready = nxt_ready

    sc.wait_all("sync", final_toks)
    sc.flush()
    return nc


def _t5_bucket(rel):
    rel = np.asarray(rel, np.int64)
    nb_half, max_exact = 16, 8
    ret = np.where(rel > 0, nb_half, 0)
    n = np.abs(rel)
    lg = (np.log(np.maximum(n, 1).astype(np.float32) / np.float32(max_exact)) / np.float32(math.log(1024 / max_exact))
          * np.float32(nb_half - max_exact)).astype(np.int32)
    large = np.minimum(max_exact + lg, nb_half - 1)
    return ret + np.where(n < max_exact, n, large)


def _constants():
    ident = np.eye(128, dtype=np.float32)
    J = ident[::-1].copy()
    ob = np.zeros((128, 128), np.float32)
    ob[:64, :64] = 1
    ob[64:, 64:] = 1
    R = np.zeros((128, 128), np.float32)
    for m in range(128):
        d = m % 64
        base = m - d
        blk = d // 32
        dd = d % 32
        if dd < 16:
            R[base + blk * 32 + dd + 16, m] = -1.0
        else:
            R[base + blk * 32 + dd - 16, m] = 1.0
    cmat = np.concatenate([ident, J, ob, R], axis=1).astype(np.float32)
    t = np.arange(S)
    row = (t // 64).astype(np.float64)
    col = (t % 64).astype(np.float64)
    inv = (10000.0 ** (-np.arange(16, dtype=np.float32) / np.float32(16))).astype(np.float32).astype(np.float64)
    C = np.zeros((128, S), np.float32)
    Sn = np.zeros((128, S), np.float32)
    for p in range(128):
        d = p % 64
        pos = row if d < 32 else col
        f = (d % 32) % 16
        ang = (pos.astype(np.float32) * np.float32(inv[f])).astype(np.float32)
        C[p] = np.cos(ang.astype(np.float64))
        Sn[p] = np.sin(ang.astype(np.float64))
    u = np.arange(UW)
    d = 1535 - u
    ad = np.abs(d)
    mult = (ad <= 64).astype(np.int64) + ((d % 4 == 0) & (ad <= 256)) + ((d % 16 == 0) & (ad <= 1024))
    mult[u >= 3071] = 0
    bkt = _t5_bucket(d)
    oh = np.zeros((33, UW), np.float32)
    valid = mult > 0
    oh[bkt[valid], u[valid]] = 1.0
    oh[32] = np.where(valid, np.log(np.maximum(mult, 1)), -30000.0)
    return cmat, C, Sn, oh


_CACHE = {}


def kernel(x, attn_norm_g, w_in, q_norm_g, k_norm_g, rel_bias, out_norm_a_g, out_norm_b_g,
           w_out, mlp_norm_g, w_up, w_down, final_norm_g):
    x = np.asarray(x, np.float32)
    f = lambda a: np.ascontiguousarray(np.asarray(a, np.float32))
    w_in0 = f(w_in)[0]
    cols = []
    for i in range(4):
        cols += list(range(64 * i, 64 * i + 64)) + list(range(64 * (i + 4), 64 * (i + 4) + 64))
    cols += list(range(512, 640))
    cols += list(range(768, 1280))
    cols += list(range(1280, 1792))
    cols += list(range(640, 768))
    cols += list(range(1792, 2304))
    w_in_p = np.ascontiguousarray(w_in0[:, np.array(cols)])
    small = np.zeros((128, 32), np.float32)
    small[:, 0:8] = f(attn_norm_g)[0].reshape(8, 128).T
    small[:, 8:16] = np.concatenate([f(out_norm_a_g)[0], f(out_norm_b_g)[0]]).reshape(8, 128).T
    small[:, 16:24] = f(mlp_norm_g)[0].reshape(8, 128).T
    small[:, 24] = np.tile(f(q_norm_g)[0], 2)
    small[:, 25] = np.tile(f(k_norm_g)[0], 2)
    rb33 = np.concatenate([f(rel_bias), np.ones((1, 8), np.float32)], axis=0)
    if "c" not in _CACHE:
        _CACHE["c"] = _constants()
    cmat, C, Sn, oh = _CACHE["c"]
    nc = build_nc()
    shared = {
        "w_in": w_in_p, "w_out": f(w_out)[0], "w_up": f(w_up)[0], "w_dn": f(w_down)[0],
        "small": small, "gfin": f(final_norm_g).reshape(1, D), "cmat": cmat, "ropeC": C, "ropeS": Sn,
        "rb33": rb33, "oh33": oh,
    }
    in_maps = []
    for c in range(NCORES):
        m = dict(shared)
        m["x"] = np.ascontiguousarray(x[2 * c:2 * c + 2])
        in_maps.append(m)
    res = run_bass_kernel_spmd(nc, in_maps, core_ids=list(range(NCORES)))
    return np.concatenate([r["out"] for r in res.results], axis=0).astype(np.float32)
```
